# Optimizing a Trainium2 kernel written in Bass

```python
import jax, jax.numpy as jnp
from jax import lax
import numpy as np

D_MODEL = 1024
BATCH = 4
SEQ = 8192
DEPTH = 2

CHUNK = 64
N_META = 16
HEAD_DIM = 64
Q_BLOCK = 128
POOL_WINDOWS = (2, 4, 8, 16)
N_POOL_GROUPS = 4
POOL_WIDTH = D_MODEL // 4
POOL_GROUP = POOL_WIDTH // N_POOL_GROUPS
FOX_HEADS = (3 * D_MODEL // 8) // HEAD_DIM
FOX_WIDTH = FOX_HEADS * HEAD_DIM
SB_HEADS = (3 * D_MODEL // 8) // HEAD_DIM
SB_WIDTH = SB_HEADS * HEAD_DIM
MIX_WIDTH = POOL_WIDTH + FOX_WIDTH + SB_WIDTH
IN_WIDTH = POOL_WIDTH + 3 * FOX_WIDTH + FOX_HEADS + 3 * SB_WIDTH
IN_SPLITS = (
    POOL_WIDTH,
    POOL_WIDTH + FOX_WIDTH,
    POOL_WIDTH + 2 * FOX_WIDTH,
    POOL_WIDTH + 3 * FOX_WIDTH,
    POOL_WIDTH + 3 * FOX_WIDTH + FOX_HEADS,
    POOL_WIDTH + 3 * FOX_WIDTH + FOX_HEADS + SB_WIDTH,
    POOL_WIDTH + 3 * FOX_WIDTH + FOX_HEADS + 2 * SB_WIDTH,
)
D_FF = ((8 * D_MODEL // 3 + 255) // 256) * 256
EPS = 1e-6

kernel_name = 'hymba_pool_fox_stickbreak_trunk'


def rmsnorm(x, g):
    x32 = x.astype(jnp.float32)
    y = x32 * lax.rsqrt(jnp.mean(x32 * x32, axis=-1, keepdims=True) + EPS)
    return (y * g.astype(jnp.float32)).astype(x.dtype)


def head_rmsnorm(y, gain, n_heads):
    b, l, _ = y.shape
    y32 = y.reshape(b, l, n_heads, HEAD_DIM).astype(jnp.float32)
    y32 = y32 * lax.rsqrt(jnp.mean(y32 * y32, axis=-1, keepdims=True) + EPS)
    return (y32.reshape(b, l, n_heads * HEAD_DIM) * gain.astype(jnp.float32)).astype(y.dtype)


def split_heads(t, n_heads):
    b, l, _ = t.shape
    return t.reshape(b, l, n_heads, HEAD_DIM).transpose(0, 2, 1, 3)


def query_blocks(t):
    l = t.shape[2]
    t = t.reshape(t.shape[:2] + (l // Q_BLOCK, Q_BLOCK) + t.shape[3:])
    return jnp.moveaxis(t, 2, 0)


def merge_blocks(o):
    nblk, b, h, q, d = o.shape
    o = jnp.moveaxis(o, 0, 2).reshape(b, h, nblk * q, d)
    return o.transpose(0, 2, 1, 3).reshape(b, nblk * q, h * d)


def multiscale_pool_mixer(u, pool_w, pool_scale):
    b, l, _ = u.shape
    ug = u.reshape(b, l, N_POOL_GROUPS, POOL_GROUP).astype(jnp.float32)
    cs = jnp.cumsum(ug, axis=1)
    cs0 = jnp.concatenate([jnp.zeros_like(cs[:, :1]), cs], axis=1)
    t1 = jnp.arange(1, l + 1)[:, None]
    win = jnp.array(POOL_WINDOWS, dtype=jnp.int32)[None, :]
    lo = jnp.maximum(t1 - win, 0)
    window_sum = cs - cs0[:, lo, jnp.arange(N_POOL_GROUPS)[None, :]]
    mean = window_sum / (t1 - lo).astype(jnp.float32)[None, :, :, None]
    d = (mean - ug).astype(u.dtype)
    mixed = jnp.einsum('blgc,gcd->blgd', d, pool_w)
    return mixed.reshape(b, l, POOL_WIDTH) * pool_scale


def forgetting_attention(q, k, v, log_f):
    c = jnp.cumsum(log_f.astype(jnp.float32), axis=1).transpose(0, 2, 1)
    l = q.shape[2]
    pos = jnp.arange(l)
    k32 = k.astype(jnp.float32)
    scale = HEAD_DIM ** -0.5

    def block(args):
        qb, cb, tq = args
        s = jnp.einsum('bhqd,bhkd->bhqk', qb.astype(jnp.float32), k32) * scale
        s = s + cb[..., None] - c[:, :, None, :]
        s = jnp.where(pos[None, :] <= tq[:, None], s, -jnp.inf)
        p = jax.nn.softmax(s, axis=-1)
        return jnp.einsum('bhqk,bhkd->bhqd', p.astype(v.dtype), v)

    o = lax.map(block, (query_blocks(q), query_blocks(c), pos.reshape(-1, Q_BLOCK)))
    return merge_blocks(o)


def stick_breaking_attention(q, k, v):
    l = q.shape[2]
    pos = jnp.arange(l)
    k32 = k.astype(jnp.float32)
    scale = HEAD_DIM ** -0.5

    def block(args):
        qb, tq = args
        z = jnp.einsum('bhqd,bhkd->bhqk', qb.astype(jnp.float32), k32) * scale
        strict = pos[None, :] < tq[:, None]
        log_beta = jax.nn.log_sigmoid(z)
        log_1m = jnp.where(strict, jax.nn.log_sigmoid(-z), 0.0)
        later = lax.cumsum(log_1m, axis=3, reverse=True) - log_1m
        a = jnp.where(strict, jnp.exp(log_beta + later), 0.0)
        return jnp.einsum('bhqk,bhkd->bhqd', a.astype(v.dtype), v)

    o = lax.map(block, (query_blocks(q), pos.reshape(-1, Q_BLOCK)))
    return merge_blocks(o)


def setup_inputs(seed: int = 0) -> dict:
    key = jax.random.key(seed)
    ks = jax.random.split(key, 15)
    f32 = jnp.float32
    n = jax.random.normal
    return {
        'x': n(ks[0], (BATCH, SEQ, D_MODEL), f32),
        'meta_tokens': n(ks[1], (N_META, D_MODEL), f32),
        'norm1': 1.0 + 0.05 * n(ks[2], (DEPTH, D_MODEL), f32),
        'w_in': n(ks[3], (DEPTH, D_MODEL, IN_WIDTH), f32) * D_MODEL ** -0.5,
        'forget_bias': jax.random.uniform(ks[4], (DEPTH, FOX_HEADS), f32, 1.0, 4.0),
        'pool_w': n(ks[5], (DEPTH, N_POOL_GROUPS, POOL_GROUP, POOL_GROUP), f32) * POOL_GROUP ** -0.5,
        'pool_scale': 1.0 + 0.1 * n(ks[6], (DEPTH, POOL_WIDTH), f32),
        'fox_out_gain': 1.0 + 0.05 * n(ks[7], (DEPTH, FOX_WIDTH), f32),
        'sb_out_gain': 1.0 + 0.05 * n(ks[8], (DEPTH, SB_WIDTH), f32),
        'w_out': n(ks[9], (DEPTH, MIX_WIDTH, D_MODEL), f32) * MIX_WIDTH ** -0.5,
        'norm2': 1.0 + 0.05 * n(ks[10], (DEPTH, D_MODEL), f32),
        'w_gate': n(ks[11], (DEPTH, D_MODEL, D_FF), f32) * D_MODEL ** -0.5,
        'w_up': n(ks[12], (DEPTH, D_MODEL, D_FF), f32) * D_MODEL ** -0.5,
        'w_down': n(ks[13], (DEPTH, D_FF, D_MODEL), f32) * D_FF ** -0.5,
        'final_norm': 1.0 + 0.05 * n(ks[14], (D_MODEL,), f32),
    }


def reference(x, meta_tokens, norm1, w_in, forget_bias, pool_w, pool_scale, fox_out_gain,
              sb_out_gain, w_out, norm2, w_gate, w_up, w_down, final_norm):
    b, s_len, _ = x.shape
    l = N_META + s_len
    l_pad = -(-l // Q_BLOCK) * Q_BLOCK
    meta = jnp.broadcast_to(meta_tokens[None].astype(x.dtype), (b, N_META, D_MODEL))
    h = jnp.concatenate([meta, x, jnp.zeros((b, l_pad - l, D_MODEL), x.dtype)], axis=1)

    for i in range(DEPTH):
        a = rmsnorm(h, norm1[i])
        proj = a @ w_in[i]
        u, qf, kf, vf, fl, qs, ks_, vs = jnp.split(proj, IN_SPLITS, axis=-1)
        y_pool = multiscale_pool_mixer(u, pool_w[i], pool_scale[i])
        log_f = jax.nn.log_sigmoid(fl.astype(jnp.float32) + forget_bias[i].astype(jnp.float32))
        y_fox = forgetting_attention(split_heads(qf, FOX_HEADS), split_heads(kf, FOX_HEADS),
                                     split_heads(vf, FOX_HEADS), log_f)
        y_fox = head_rmsnorm(y_fox, fox_out_gain[i], FOX_HEADS)
        y_sb = stick_breaking_attention(split_heads(qs, SB_HEADS), split_heads(ks_, SB_HEADS),
                                        split_heads(vs, SB_HEADS))
        y_sb = head_rmsnorm(y_sb, sb_out_gain[i], SB_HEADS)
        h = h + jnp.concatenate([y_pool, y_fox, y_sb], axis=-1) @ w_out[i]
        a = rmsnorm(h, norm2[i])
        h = h + (jax.nn.silu(a @ w_gate[i]) * (a @ w_up[i])) @ w_down[i]

    h = rmsnorm(h, final_norm)
    return h[:, N_META:N_META + s_len]
```

```python
import math
from contextlib import ExitStack

import numpy as np
import concourse.bass as bass
import concourse.mybir as mybir
from concourse.bass_utils import run_bass_kernel_spmd

F32 = mybir.dt.float32
BF16 = mybir.dt.bfloat16
AF = mybir.ActivationFunctionType
ALU = mybir.AluOpType

D = 1024
NH = 6
HD = 64
PW = 256
DFF = 2816
INW = 2566
NMETA = 16
SEQ = 8192
BATCH = 4
DEPTH = 2
EPS = 1e-6
LFULL = 8320
NEG = -30000.0

C_U, C_FQ, C_FK, C_FV, C_FL, C_SQ, C_SK, C_SV = 0, 256, 640, 1024, 1408, 1414, 1798, 2182


EMBED_WAIT = True
STRICT_SAME_ENGINE = True


class Buf:
    __slots__ = ("name", "w", "r")

    def __init__(self, name):
        self.name = name
        self.w = None
        self.r = {}


class Prog:
    ENG = ("pe", "act", "dve", "pool", "sp")

    def __init__(self, nc, stack):
        self.nc = nc
        self.stack = stack
        self.ops = {e: [] for e in self.ENG}
        self.sem = {}
        self.cnt = {}
        self.waited = {e: {} for e in self.ENG}
        for e in self.ENG:
            self._mk(e)

    def _mk(self, key):
        self.sem[key] = self.stack.enter_context(self.nc.semaphore("s_" + key))
        self.cnt[key] = 0

    def _need(self, eng, tok, need):
        if tok is None:
            return
        key, val = tok
        if key == eng and eng == "pe":
            return
        if key == eng and self.cnt[eng] - val >= 2 and not STRICT_SAME_ENGINE:
            return
        if self.waited[eng].get(key, 0) >= val:
            return
        if need.get(key, 0) < val:
            need[key] = val

    def op(self, eng, fn, reads=(), writes=(), dma=None, signal=True):
        need = {}
        for b in reads:
            self._need(eng, b.w, need)
        for b in writes:
            self._need(eng, b.w, need)
            for k, v in b.r.items():
                if k == eng and not (STRICT_SAME_ENGINE and eng != "pe"):
                    continue
                self._need(eng, (k, v), need)
        for key, val in need.items():
            assert val <= self.cnt[key], f"wait on unsignaled token {key} {val} > {self.cnt[key]}"
            self.ops[eng].append(("w", key, val))
            self.waited[eng][key] = val
        if dma is not None:
            if dma not in self.sem:
                self._mk(dma)
            self.cnt[dma] += 16
            tok = (dma, self.cnt[dma])
            self.ops[eng].append(("i", fn, dma, 16))
        elif signal:
            self.cnt[eng] += 1
            tok = (eng, self.cnt[eng])
            self.ops[eng].append(("i", fn, eng, 1))
        else:
            tok = (eng, self.cnt[eng] + 1)
            self.ops[eng].append(("i", fn, None, 0))
        for b in reads:
            if b.r.get(tok[0], 0) < tok[1]:
                b.r[tok[0]] = tok[1]
        for b in writes:
            b.w = tok
            b.r = {}

    def barrier(self):
        for e in self.ENG:
            for key, val in self.cnt.items():
                if val > 0 and self.waited[e].get(key, 0) < val and key != e:
                    self.ops[e].append(("w", key, val))
                    self.waited[e][key] = val

    def emit(self):
        nc = self.nc
        sem = self.sem

        def replay(name, e):
            pend = []
            for it in self.ops[name]:
                if it[0] == "w":
                    pend.append(it)
                else:
                    emb = None
                    if EMBED_WAIT and pend and name != "sp" and it[3] != 16:
                        emb = pend.pop()
                    for w_ in pend:
                        e.wait_ge(sem[w_[1]], w_[2])
                    pend = []
                    ins = it[1](e)
                    if emb is not None:
                        ins._wait_ge(sem[emb[1]], emb[2])
                    if it[2] is not None:
                        ins.then_inc(sem[it[2]], it[3])
            for w_ in pend:
                e.wait_ge(sem[w_[1]], w_[2])

        with nc.Block() as block:
            @block.tensor
            def _(e):
                replay("pe", e)

            @block.scalar
            def _(e):
                replay("act", e)

            @block.vector
            def _(e):
                replay("dve", e)

            @block.gpsimd
            def _(e):
                replay("pool", e)

            @block.sync
            def _(e):
                replay("sp", e)

    def mm(self, out, lhsT, rhs, start=True, stop=True, r=(), w=(), signal=True, skip=False):
        def fn(e):
            return e.matmul(out, lhsT, rhs, start=start, stop=stop, skip_group_check=skip)
        self.op("pe", fn, r, w, signal=signal)

    def tr(self, out, in_, ident, r=(), w=(), signal=True):
        self.op("pe", lambda e: e.transpose(out, in_, ident), r, w, signal=signal)

    def act(self, out, in_, func, r=(), w=(), bias=None, scale=None, accum=None):
        kw = {}
        if bias is not None:
            kw["bias"] = bias
        if scale is not None:
            kw["scale"] = scale
        if accum is not None:
            kw["accum_out"] = accum
        self.op("act", lambda e: e.activation(out, in_, func, **kw), r, w)

    def v(self, eng, name, *args, r=(), w=(), **kw):
        self.op(eng, lambda e: getattr(e, name)(*args, **kw), r, w)

    def dma(self, out, in_, sem, r=(), w=(), q="sp"):
        self.op(q, lambda e: e.dma_start(out=out, in_=in_), r, w, dma=sem)


def _const_layout():
    names = ["ident", "tri_le", "ones", "nmf", "nms", "pms", "tri_ge", "ntri_ge", "nones"]
    for g in range(4):
        names += [f"pd{g}", f"pp{g}", f"pz{g}"]
    off = {}
    o = 0
    for n in names:
        off[n] = o
        o += 128
    return off, o


def make_consts():
    off, n = _const_layout()
    c = np.zeros((128, n), np.float32)
    j = np.arange(128)[:, None]
    i = np.arange(128)[None, :]
    c[:, off["ident"]:off["ident"] + 128] = (j == i)
    c[:, off["tri_le"]:off["tri_le"] + 128] = (j <= i)
    c[:, off["ones"]:off["ones"] + 128] = 1.0
    c[:, off["nmf"]:off["nmf"] + 128] = np.where(j > i, NEG, 0.0)
    c[:, off["nms"]:off["nms"] + 128] = np.where(j >= i, NEG, 0.0)
    c[:, off["pms"]:off["pms"] + 128] = np.where(j >= i, -NEG, 0.0)
    c[:, off["tri_ge"]:off["tri_ge"] + 128] = (j >= i)
    c[:, off["ntri_ge"]:off["ntri_ge"] + 128] = -1.0 * (j >= i)
    c[:, off["nones"]:off["nones"] + 128] = -1.0
    for g, wnd in enumerate((2, 4, 8, 16)):
        pd = np.where((j <= i) & (j > i - wnd), 1.0 / wnd, 0.0) - (j == i)
        pp = np.where((j - 128 > i - wnd), 1.0 / wnd, 0.0)
        cnt = np.minimum(i + 1, wnd).astype(np.float64)
        pz = np.where((j <= i) & (j > i - wnd), 1.0 / cnt, 0.0) - (j == i)
        c[:, off[f"pd{g}"]:off[f"pd{g}"] + 128] = pd
        c[:, off[f"pp{g}"]:off[f"pp{g}"] + 128] = pp
        c[:, off[f"pz{g}"]:off[f"pz{g}"] + 128] = pz
    return c


def build_program(L=LFULL, depth=DEPTH, dbg=False):
    assert L % 128 == 0
    NT = L // 128
    NG = (L + 511) // 512
    coff, ncon = _const_layout()

    nc = bass.Bass("TRN2", target_bir_lowering=False)
    kind_dbg = "ExternalOutput" if dbg else "Internal"

    def din(name, shape, dt=F32):
        return nc.dram_tensor(name, list(shape), dt, kind="ExternalInput").ap()

    def dscr(name, shape, dt):
        return nc.dram_tensor(name, list(shape), dt, kind=kind_dbg).ap()

    xpad = din("xpad", [L, D])
    w_in = din("w_in", [depth, D, INW])
    w_out = din("w_out", [depth, D, D])
    w_gate = din("w_gate", [depth, D, DFF])
    w_up = din("w_up", [depth, D, DFF])
    w_down = din("w_down", [depth, DFF, D])
    pool_w = din("pool_w", [depth, 64, 4, 64])
    norm1_b = din("norm1_b", [depth, 128, D])
    norm2_b = din("norm2_b", [depth, 128, D])
    final_b = din("final_b", [128, D])
    fb_b = din("fb_b", [depth, 128, NH])
    pscale_c = din("pscale_c", [depth, 64, 4])
    fgain_c = din("fgain_c", [depth, 64, NH])
    sgain_c = din("sgain_c", [depth, 64, NH])
    consts = din("consts", [128, ncon])
    out = nc.dram_tensor("out", [SEQ if L == LFULL else L, D], F32, kind="ExternalOutput").ap()
    NOUT = out.shape[0]

    Hd = dscr("Hd", [L, D], F32)
    QK = dscr("QK", [4, 384, L], BF16)
    QFc = dscr("QFc", [NH, L], BF16)
    YT = dscr("YT", [D, L], BF16)
    A2T = dscr("A2T", [D, L], BF16)
    NCK = dscr("NCK", [NH, 3, L], BF16)

    with ExitStack() as gstack:
        P = Prog(nc, gstack)

        uid = [0]

        def sb(stack, name, shape, dt):
            uid[0] += 1
            return stack.enter_context(nc.sbuf_tensor(f"{name}_{uid[0]}", list(shape), dt))

        def ps(stack, name, shape, dt=F32):
            uid[0] += 1
            return stack.enter_context(nc.psum_tensor(f"{name}_{uid[0]}", list(shape), dt))

        cf = sb(gstack, "cf", [128, 384], F32)
        cb = sb(gstack, "cb", [128, ncon], BF16)
        WCH = 512
        wst = [sb(gstack, f"wst{i}", [128, WCH], F32) for i in range(2)]
        B_wst = [Buf(f"wst{i}") for i in range(2)]
        B_c = Buf("consts")
        zero_bf = sb(gstack, "zero_bf", [128, 512], BF16)
        onesrow = sb(gstack, "onesrow", [128, 128], F32)
        ones64 = sb(gstack, "ones64", [128, 128], F32)
        B_misc = Buf("misc")

        P.dma(cf[:], consts[:, 0:384], "c0", w=[B_c])
        for c0_ in range(0, ncon, WCH):
            cw_ = min(WCH, ncon - c0_)
            i_ = (c0_ // WCH) % 2
            P.dma(wst[i_][:, 0:cw_], consts[:, c0_:c0_ + cw_], f"wst{i_}", w=[B_wst[i_]])
            P.v("dve", "tensor_copy", cb[:, c0_:c0_ + cw_], wst[i_][:, 0:cw_], r=[B_wst[i_]], w=[B_c])
        P.v("pool", "memset", zero_bf[:], 0.0, w=[B_misc])
        P.v("pool", "memset", onesrow[:], 1.0, w=[B_misc])
        P.v("pool", "memset", ones64[:], 1.0 / 64.0, w=[B_misc])
        fxn = sb(gstack, "fxn", [128, 128], F32)
        P.v("pool", "memset", fxn[:], 1.0 / 64.0, w=[B_misc])
        P.v("pool", "memset", fxn[64:65, :], 1.0, w=[B_misc])
        epsc = sb(gstack, "epsc", [128, 1], F32)
        P.v("pool", "memset", epsc[:], EPS, w=[B_misc])

        def cfs(name, rows=128, cols=128):
            o = coff[name]
            return cf[0:rows, o:o + cols]

        def cbs(name, rows=128, cols=128):
            o = coff[name]
            return cb[0:rows, o:o + cols]

        wcount = [0]

        def load_weight_gen(dst, B_dst, src, kchunks, ncols, col0=0, dst_col0=0, pool_only=False):
            for k in range(kchunks):
                for cc in range(0, ncols, WCH):
                    cw = min(WCH, ncols - cc)
                    i = wcount[0] % 2
                    wcount[0] += 1
                    P.dma(wst[i][:, 0:cw], src[k * 128:(k + 1) * 128, col0 + cc:col0 + cc + cw],
                          f"wst{i}", w=[B_wst[i]])
                    eng = "pool" if (pool_only or ((wcount[0] // 2) % 2)) else "dve"
                    P.v(eng, "tensor_copy", dst[:, k, dst_col0 + cc:dst_col0 + cc + cw], wst[i][:, 0:cw],
                        r=[B_wst[i]], w=[B_dst])
                    yield

        def load_weight(dst, B_dst, src, kchunks, ncols, col0=0, dst_col0=0):
            for _ in load_weight_gen(dst, B_dst, src, kchunks, ncols, col0, dst_col0):
                pass

        for layer in range(depth):
            hsrc = xpad if layer == 0 else Hd
            last = layer == depth - 1
            with ExitStack() as astack:
                vf_all = sb(astack, "vf_all", [128, NT, 5 * 65 + 128 + 3], BF16)
                vs_all = sb(astack, "vs_all", [128, NT, NH * 64], BF16)
                negc_all = sb(astack, "negc_all", [128, NT, NH], F32)
                B_vf = [Buf(f"vf{t}") for t in range(NT)]
                B_vs = [Buf(f"vs{t}") for t in range(NT)]
                B_negc = [Buf(f"negc{t}") for t in range(NT)]
                B_vinit = Buf("vinit")
                P.v("pool", "memset", vf_all[:], 1.0, w=[B_vinit])
                for b in B_vf:
                    b.w = B_vinit.w

                with ExitStack() as st:
                    wt = sb(st, "wt", [128, 8, 1030], BF16)
                    wf = sb(st, "wf", [128, 8, 1536], BF16)
                    pw = sb(st, "pw", [64, 4, 64], BF16)
                    pwf = sb(st, "pwf", [64, 4, 64], F32)
                    g1 = sb(st, "g1", [128, D], F32)
                    fbt = sb(st, "fbt", [128, NH], F32)
                    psc = sb(st, "psc", [64, 4], F32)
                    B_wt, B_wf, B_small = Buf("wt"), Buf("wf"), Buf("small1")
                    load_weight(wt, B_wt, w_in[layer], 8, 256, C_U, 0)
                    load_weight(wt, B_wt, w_in[layer], 8, 384, C_FV, 256)
                    load_weight(wt, B_wt, w_in[layer], 8, 384, C_SV, 640)
                    load_weight(wt, B_wt, w_in[layer], 8, 6, C_FL, 1024)
                    load_weight(wf, B_wf, w_in[layer], 8, 768, C_FQ, 0)
                    load_weight(wf, B_wf, w_in[layer], 8, 768, C_SQ, 768)
                    P.dma(g1[:], norm1_b[layer], "c1", w=[B_small])
                    P.dma(fbt[:], fb_b[layer], "c2", w=[B_small])
                    P.dma(psc[:], pscale_c[layer], "c3", w=[B_small])
                    P.dma(pwf[:], pool_w[layer], "c4", w=[B_small])
                    P.v("dve", "tensor_copy", pw[:], pwf[:], r=[B_small], w=[B_small])

                    xt = [sb(st, f"xt{i}", [128, D], F32) for i in range(2)]
                    B_xt = [Buf(f"xt{i}") for i in range(2)]
                    junk = sb(st, "junk", [128, D], BF16)
                    B_junk = Buf("junk")
                    ss = [sb(st, f"ss{i}", [128, 2], F32) for i in range(2)]
                    B_ss = [Buf(f"ss{i}") for i in range(2)]
                    abf = [sb(st, f"abf{i}", [128, D], BF16) for i in range(2)]
                    B_abf = [Buf(f"abf{i}") for i in range(2)]
                    aTg = [sb(st, f"aTg{i}", [128, 8, 512], BF16) for i in range(2)]
                    B_aTg = [Buf(f"aTg{i}") for i in range(2)]
                    ubf = [sb(st, f"ubf{i}", [128, 256], BF16) for i in range(2)]
                    B_ubf = [Buf(f"ubf{i}") for i in range(2)]
                    flt = [sb(st, f"flt{i}", [128, 2 * NH], F32) for i in range(2)]
                    B_flt = [Buf(f"flt{i}") for i in range(2)]
                    carry = [sb(st, f"carry{i}", [128, NH], F32) for i in range(2)]
                    B_carry = [Buf(f"carry{i}") for i in range(2)]
                    cst = [sb(st, f"cst{i}", [NH, 512], BF16) for i in range(2)]
                    B_cst = [Buf(f"cst{i}") for i in range(2)]
                    qst = [sb(st, f"qst{i}", [128, 512], BF16) for i in range(3)]
                    B_qst = [Buf(f"qst{i}") for i in range(3)]
                    dT = [sb(st, f"dT{i}", [64, 4, 128], BF16) for i in range(2)]
                    B_dT = [Buf(f"dT{i}") for i in range(2)]
                    ypl = [sb(st, f"ypl{i}", [64, 4, 128], BF16) for i in range(2)]
                    B_ypl = [Buf(f"ypl{i}") for i in range(2)]
                    ps_t = [ps(st, f"ps_t{i}", [128, 8, 128], BF16) for i in range(1)]
                    B_pst = [Buf(f"ps_t{i}") for i in range(1)]
                    ps_a = [ps(st, f"ps_a{i}", [128, 512]) for i in range(3)]
                    B_psa = [Buf(f"ps_a{i}") for i in range(3)]
                    ps_f = [ps(st, f"ps_f{i}", [128, 512]) for i in range(2)]
                    B_psf = [Buf(f"ps_f{i}") for i in range(2)]
                    ps_m = [ps(st, f"ps_m{i}", [128, 512]) for i in range(2)]
                    B_psm = [Buf(f"ps_m{i}") for i in range(2)]

                    P.v("pool", "memset", carry[0][:], 0.0, w=[B_carry[0]])

                    def p1_load(t):
                        i = t % 2
                        P.dma(xt[i][:], hsrc[t * 128:(t + 1) * 128, :], f"xt{i}", w=[B_xt[i]])

                    p1_load(0)
                    fcount = 0

                    def p1_front(t):
                        i = t % 2
                        gi = (t // 4) % 2
                        j = t % 4
                        if t + 1 < NT:
                            p1_load(t + 1)
                        P.act(junk[:], xt[i][:], AF.Square, r=[B_xt[i]], w=[B_junk, B_ss[i]],
                              accum=ss[i][:, 0:1])
                        P.act(ss[i][:, 1:2], ss[i][:, 0:1], AF.Ln, r=[B_ss[i]], w=[B_ss[i]], scale=1.0 / D, bias=epsc[:, 0:1])
                        P.act(ss[i][:, 1:2], ss[i][:, 1:2], AF.Exp, r=[B_ss[i]], w=[B_ss[i]], scale=-0.5)
                        P.v("dve", "scalar_tensor_tensor", abf[i][:], xt[i][:], ss[i][:, 1:2], g1[:],
                            ALU.mult, ALU.mult, r=[B_xt[i], B_ss[i], B_small], w=[B_abf[i]])
                        for k in range(8):
                            P.tr(ps_t[0][:, k, :], abf[i][:, k * 128:(k + 1) * 128], cbs("ident"),
                                 r=[B_abf[i], B_c], w=[B_pst[0]], signal=(k == 7))
                        P.act(aTg[gi][:, :, j * 128:(j + 1) * 128], ps_t[0][:], AF.Copy,
                              r=[B_pst[0]], w=[B_aTg[gi]])

                    def p1_back(t):
                        nonlocal fcount
                        i = t % 2
                        gi = (t // 4) % 2
                        j = t % 4
                        aT = aTg[gi]
                        for k in range(8):
                            P.mm(ps_a[0][:, 0:256], aT[:, k, j * 128:(j + 1) * 128], wt[:, k, 0:256],
                                 start=(k == 0), stop=(k == 7), r=[B_aTg[gi], B_wt], w=[B_psa[0]],
                                 signal=False)
                        for k in range(8):
                            P.mm(ps_a[0][:, 256:262], aT[:, k, j * 128:(j + 1) * 128], wt[:, k, 1024:1030],
                                 start=(k == 0), stop=(k == 7), r=[B_aTg[gi], B_wt], w=[B_psa[0]],
                                 signal=(k == 7), skip=True)
                        for k in range(8):
                            P.mm(ps_a[1][:, 0:384], aT[:, k, j * 128:(j + 1) * 128], wt[:, k, 256:640],
                                 start=(k == 0), stop=(k == 7), r=[B_aTg[gi], B_wt], w=[B_psa[1]],
                                 signal=(k == 7))
                        for k in range(8):
                            P.mm(ps_a[2][:, 0:384], aT[:, k, j * 128:(j + 1) * 128], wt[:, k, 640:1024],
                                 start=(k == 0), stop=(k == 7), r=[B_aTg[gi], B_wt], w=[B_psa[2]],
                                 signal=(k == 7))
                        P.act(ubf[i][:], ps_a[0][:, 0:256], AF.Copy, r=[B_psa[0]], w=[B_ubf[i]])
                        P.v("dve", "tensor_tensor", flt[i][:, 0:NH], ps_a[0][:, 256:262], fbt[:], ALU.add,
                            r=[B_psa[0], B_small], w=[B_flt[i]])
                        P.v("dve", "tensor_copy",
                            vf_all[:, t, 0:NH * 65].rearrange("p (h d) -> p h d", d=65)[:, :, 0:64],
                            ps_a[1][:, 0:384].rearrange("p (h d) -> p h d", d=64),
                            r=[B_psa[1]], w=[B_vf[t]])
                        P.act(vs_all[:, t, :], ps_a[2][:, 0:384], AF.Copy, r=[B_psa[2]], w=[B_vs[t]])
                        P.act(flt[i][:, 0:NH], flt[i][:, 0:NH], AF.Exp, r=[B_flt[i]], w=[B_flt[i]], scale=-1.0)
                        P.act(flt[i][:, NH:2 * NH], flt[i][:, 0:NH], AF.Ln, r=[B_flt[i]], w=[B_flt[i]], bias=1.0)
                        m = ps_m[t % 2]
                        B_m = B_psm[t % 2]
                        P.mm(m[:, 0:NH], cfs("tri_le"), flt[i][:, NH:2 * NH], r=[B_c, B_flt[i]], w=[B_m],
                             signal=False)
                        P.mm(m[:, NH:2 * NH], cfs("ones"), flt[i][:, NH:2 * NH], r=[B_c, B_flt[i]], w=[B_m],
                             skip=True)
                        ci, co = t % 2, (t + 1) % 2
                        P.v("dve", "tensor_tensor", negc_all[:, t, :], m[:, 0:NH], carry[ci][:], ALU.add,
                            r=[B_m, B_carry[ci]], w=[B_negc[t]])
                        P.v("dve", "tensor_tensor", carry[co][:], m[:, NH:2 * NH], carry[ci][:], ALU.add,
                            r=[B_m, B_carry[ci]], w=[B_carry[co]])
                        P.tr(m[0:NH, 128:256], negc_all[:, t, :], cfs("ident"), r=[B_negc[t], B_c], w=[B_m])
                        P.act(cst[gi][:, j * 128:(j + 1) * 128], m[0:NH, 128:256], AF.Copy,
                              r=[B_m], w=[B_cst[gi]], scale=-1.0)
                        ip = (t + 1) % 2
                        group_end = (j == 3) or (t == NT - 1)
                        pd_ps = ps_f[fcount % 2]
                        B_pd = B_psf[fcount % 2]
                        fcount += 1
                        for g in range(4):
                            if t == 0:
                                P.mm(pd_ps[0:64, g * 128:(g + 1) * 128], ubf[i][:, g * 64:(g + 1) * 64],
                                     cbs("pz%d" % g), r=[B_ubf[i], B_c], w=[B_pd], signal=(g == 3), skip=True)
                            else:
                                P.mm(pd_ps[0:64, g * 128:(g + 1) * 128], ubf[i][:, g * 64:(g + 1) * 64],
                                     cbs("pd%d" % g), start=True, stop=False, r=[B_ubf[i], B_c], w=[B_pd],
                                     signal=False, skip=True)
                                P.mm(pd_ps[0:64, g * 128:(g + 1) * 128], ubf[ip][:, g * 64:(g + 1) * 64],
                                     cbs("pp%d" % g), start=False, stop=True, r=[B_ubf[ip], B_c], w=[B_pd],
                                     signal=(g == 3), skip=True)
                        P.v("dve", "tensor_copy", dT[i][:].rearrange("p g t -> p (g t)"), pd_ps[0:64, 0:512],
                            r=[B_pd], w=[B_dT[i]])
                        py_ps = ps_f[fcount % 2]
                        B_py = B_psf[fcount % 2]
                        fcount += 1
                        for g in range(4):
                            P.mm(py_ps[0:64, g * 128:(g + 1) * 128], pw[:, g, :], dT[i][:, g, :],
                                 r=[B_small, B_dT[i]], w=[B_py], signal=(g == 3), skip=True)
                        for g in range(4):
                            P.act(ypl[i][:, g, :], py_ps[0:64, g * 128:(g + 1) * 128], AF.Identity,
                                  r=[B_py, B_small], w=[B_ypl[i]], scale=psc[:, g:g + 1])
                        P.dma(YT[0:256, t * 128:(t + 1) * 128].rearrange("(g d) t -> d g t", d=64), ypl[i][:],
                              f"ypl{i}", r=[B_ypl[i]], q="pool")
                        if group_end:
                            gw = (j + 1) * 128
                            g0 = (t // 4) * 512
                            P.dma(QFc[:, g0:g0 + gw], cst[gi][:, 0:gw], f"cst{gi}", r=[B_cst[gi]], q="pool")
                            for mt in range(12):
                                pf = ps_f[fcount % 2]
                                B_pf = B_psf[fcount % 2]
                                fcount += 1
                                for k in range(8):
                                    P.mm(pf[:, 0:gw], wf[:, k, mt * 128:(mt + 1) * 128], aT[:, k, 0:gw],
                                         start=(k == 0), stop=(k == 7), r=[B_wf, B_aTg[gi]], w=[B_pf],
                                         signal=(k == 7))
                                kind, mm_ = mt // 3, mt % 3
                                qi = mt % 3
                                scale = 0.125 if kind in (0, 2) else 1.0
                                if mt % 2 == 0:
                                    P.act(qst[qi][:, 0:gw], pf[:, 0:gw], AF.Copy, r=[B_pf], w=[B_qst[qi]],
                                          scale=scale)
                                else:
                                    P.v("dve", "tensor_scalar", qst[qi][:, 0:gw], pf[:, 0:gw], scale, None,
                                        ALU.mult, r=[B_pf], w=[B_qst[qi]])
                                P.dma(QK[kind, mm_ * 128:(mm_ + 1) * 128, g0:g0 + gw], qst[qi][:, 0:gw],
                                      f"qst{qi}", r=[B_qst[qi]], q="pool")
                    p1_front(0)
                    for t in range(NT):
                        if t + 1 < NT:
                            p1_front(t + 1)
                        p1_back(t)
                P.barrier()

                with ExitStack() as st:
                    KB = [sb(st, f"KB{i}", [128, L], BF16) for i in range(2)]
                    QB = [sb(st, f"QB{i}", [128, L], BF16) for i in range(2)]
                    B_KB = [Buf(f"KB{i}") for i in range(2)]
                    B_QB = [Buf(f"QB{i}") for i in range(2)]
                    fg = sb(st, "fg", [128, NH], F32)
                    sg = sb(st, "sg", [128, NH], F32)
                    B_g = Buf("gains")
                    P.dma(fg[0:64, :], fgain_c[layer], "c1", w=[B_g])
                    P.dma(sg[0:64, :], sgain_c[layer], "c2", w=[B_g])
                    P.dma(sg[64:128, :], sgain_c[layer], "c3", w=[B_g])
                    for i in range(2):
                        P.v("pool", "memset", KB[i][:, :], 0.0, w=[B_KB[i]])
                        P.v("pool", "memset", QB[i][:, :], 0.0, w=[B_QB[i]])
                        P.v("pool", "memset", QB[i][64:68, :], 1.0, w=[B_QB[i]])
                        P.v("pool", "memset", KB[i][64:65, :], 1.0, w=[B_KB[i]])
                    pT = [sb(st, f"pT{i}", [128, 512], BF16) for i in range(3)]
                    B_pT = [Buf(f"pT{i}") for i in range(3)]
                    ee = [sb(st, f"ee{i}", [128, 512], F32) for i in range(2)]
                    B_ee = [Buf(f"ee{i}") for i in range(2)]
                    spb = [sb(st, f"spb{i}", [128, 512], BF16) for i in range(3)]
                    B_spb = [Buf(f"spb{i}") for i in range(3)]
                    sacc = [[sb(st, f"sacc{i}_{k}", [128, 512], BF16) for k in range(2)] for i in range(2)]
                    B_sacc = [[Buf(f"sacc{i}_{k}") for k in range(2)] for i in range(2)]
                    negq = [None, None]
                    B_negq = [None, None]
                    osb_ = sb(st, "osb", [128, 512], F32)
                    osb = [osb_, osb_]
                    B_osb_ = Buf("osb")
                    B_osb = [B_osb_, B_osb_]
                    rden = sb(st, "rden", [128, 512], F32)
                    ysq = sb(st, "ysq", [128, 512], F32)
                    rstd = rden
                    B_rden, B_ysq = Buf("rden"), Buf("ysq")
                    B_ysq2 = Buf("ysq2")
                    B_rstd = B_rden
                    yst_ = sb(st, "yst", [128, 512], BF16)
                    yst = [yst_, yst_]
                    B_yst_ = Buf("yst")
                    B_yst = [B_yst_, B_yst_]
                    ps_s = [ps(st, f"ps_s{i}", [128, 512]) for i in range(3)]
                    B_pss = [Buf(f"ps_s{i}") for i in range(3)]
                    ps_d = [ps(st, f"ps_d{i}", [128, 512]) for i in range(2)]
                    B_psd = [Buf(f"ps_d{i}") for i in range(2)]
                    ps_o = [ps(st, f"ps_o{i}", [128, 512]) for i in range(2)]
                    B_pso = [Buf(f"ps_o{i}") for i in range(2)]
                    ps_n_ = ps(st, "ps_n", [128, 512])
                    ps_n = [ps_n_, ps_n_]
                    B_psn_ = Buf("ps_n")
                    B_psn = [B_psn_, B_psn_]

                    B_nck = Buf("nck")

                    def head_load(hh):
                        i = hh % 2
                        if hh < NH:
                            P.tr(ps_n_[0:NT, 0:128], negc_all[:, :, hh], cfs("ident"), r=[B_c], w=[B_psn_])
                            P.act(yst_[0:NT, 0:128], ps_n_[0:NT, 0:128], AF.Copy, r=[B_psn_], w=[B_yst_])
                            P.v("dve", "tensor_tensor", rden[0:NT, 0:128], ps_n_[0:NT, 0:128], yst_[0:NT, 0:128],
                                ALU.subtract, r=[B_psn_, B_yst_], w=[B_rden])
                            P.v("dve", "tensor_copy", yst_[0:NT, 128:256], rden[0:NT, 0:128], r=[B_rden],
                                w=[B_yst_])
                            P.v("dve", "tensor_tensor", yst_[0:NT, 256:384], rden[0:NT, 0:128],
                                yst_[0:NT, 128:256], ALU.subtract, r=[B_rden], w=[B_yst_])
                            for j_ in range(3):
                                P.dma(NCK[hh, j_, :].rearrange("(t p) -> t p", p=128),
                                      yst_[0:NT, j_ * 128:(j_ + 1) * 128], "nck", r=[B_yst_], w=[B_nck])
                            P.dma(KB[i][65:68, :], NCK[hh, :, :], f"KB{i}", r=[B_nck], w=[B_KB[i]])
                            P.dma(QB[i][0:64, :], QK[0, hh * 64:(hh + 1) * 64, :], f"QB{i}", w=[B_QB[i]])
                            P.dma(QB[i][64:65, :], QFc[hh:hh + 1, :], f"QB{i}", w=[B_QB[i]])
                            P.dma(KB[i][0:64, :], QK[1, hh * 64:(hh + 1) * 64, :], f"KB{i}", w=[B_KB[i]])
                        else:
                            h2 = hh - NH
                            if h2 < 2:
                                P.v("pool", "memset", QB[i][64:68, :], 0.0, w=[B_QB[i]])
                            P.dma(QB[i][0:64, :], QK[2, h2 * 64:(h2 + 1) * 64, :], f"QB{i}", w=[B_QB[i]])
                            P.dma(KB[i][0:64, :], QK[3, h2 * 64:(h2 + 1) * 64, :], f"KB{i}", w=[B_KB[i]])

                    HN_SKEW = 3

                    def head_norm_pre(o_ps, B_o, W, fox, ro=0):
                        nrows = 65 if fox else 64
                        R = slice(ro, ro + 64)
                        P.v("dve", "tensor_copy", osb_[ro:ro + nrows, 0:W], o_ps[ro:ro + nrows, 0:W], r=[B_o],
                            w=[B_osb_])
                        oth = 64 - ro
                        P.v("pool", "memset", ysq[oth:oth + 64, 0:W], 0.0, w=[B_ysq2])
                        P.v("pool", "tensor_tensor", ysq[R, 0:W], osb_[R, 0:W], osb_[R, 0:W], ALU.mult,
                            r=[B_osb_], w=[B_ysq])
                        if fox:
                            P.v("dve", "scalar_tensor_tensor", ysq[64:65, 0:W], osb_[64:65, 0:W], EPS,
                                osb_[64:65, 0:W], ALU.mult, ALU.mult, r=[B_osb_], w=[B_ysq2])

                    def head_norm_post(W, gains, h, yrow0, q0, fox, ro=0):
                        R = slice(ro, ro + 64)
                        n1 = ps_n_
                        if fox:
                            P.mm(n1[:, 0:W], fxn[0:128, 0:128], ysq[0:128, 0:W], r=[B_misc, B_ysq, B_ysq2],
                                 w=[B_psn_])
                            P.act(rstd[R, 0:W], n1[R, 0:W], AF.Ln, r=[B_psn_], w=[B_rstd])
                        else:
                            P.mm(n1[:, 0:W], ones64[0:128, 0:128], ysq[0:128, 0:W], r=[B_misc, B_ysq, B_ysq2],
                                 w=[B_psn_])
                            P.act(rstd[R, 0:W], n1[R, 0:W], AF.Ln, r=[B_psn_], w=[B_rstd], bias=epsc[R, 0:1])
                        P.act(rstd[R, 0:W], rstd[R, 0:W], AF.Exp, r=[B_rstd], w=[B_rstd], scale=-0.5)
                        P.v("dve", "scalar_tensor_tensor", yst_[R, 0:W], osb_[R, 0:W], gains[R, h:h + 1],
                            rstd[R, 0:W], ALU.mult, ALU.mult, r=[B_osb_, B_rstd, B_g], w=[B_yst_])
                        P.dma(YT[yrow0:yrow0 + 64, q0:q0 + W], yst_[R, 0:W], "yst0", r=[B_yst_])

                    head_load(0)
                    tiles = []
                    tcount = 0
                    ncount = 0
                    for hh in range(2 * NH):
                        hi = hh % 2
                        K_, Q_ = KB[hi], QB[hi]
                        BK, BQ = B_KB[hi], B_QB[hi]
                        fox = hh < NH
                        h = hh if fox else hh - NH
                        head_tile_idx = 0
                        for qb in range(NG):
                            q0 = qb * 512
                            W = min(512, L - q0)
                            nkb = (q0 + W) // 128
                            gq = (hh * NG + qb) % 2
                            o_ps = ps_o[gq]
                            B_o = B_pso[gq]
                            sa2 = sacc[gq]
                            B_sa2 = B_sacc[gq]
                            nq = negq[gq]
                            B_nq = B_negq[gq]
                            order = list(range(nkb)) if fox else list(range(nkb - 1, -1, -1))
                            for oi_, kb in enumerate(order):
                                c0 = max(0, kb * 128 - q0)
                                Wp = W - c0
                                diag = kb * 128 >= q0
                                firstt = oi_ == 0
                                lastt = oi_ == nkb - 1
                                sa, B_sa = sa2[oi_ % 2], B_sa2[oi_ % 2]
                                san, B_san = sa2[(oi_ + 1) % 2], B_sa2[(oi_ + 1) % 2]
                                si = tcount % 3
                                ei = tcount % 2
                                pi = tcount % 3
                                tcount += 1
                                s_ps, B_s = ps_s[si], B_pss[si]
                                d_ps, B_d = ps_d[ei], B_psd[ei]
                                pre_load = (hh + 1) if (head_tile_idx == 8 and hh + 1 < 2 * NH) else None
                                head_tile_idx += 1
                                nc_here = ncount
                                if lastt:
                                    ncount += 1

                                def t_A(K_=K_, Q_=Q_, BK=BK, BQ=BQ, kb=kb, q0=q0, c0=c0, W=W, Wp=Wp,
                                        diag=diag, s_ps=s_ps, B_s=B_s, pre_load=pre_load, fox=fox):
                                    if pre_load is not None:
                                        head_load(pre_load)
                                    P.mm(s_ps[:, 0:Wp], K_[0:128, kb * 128:(kb + 1) * 128],
                                         Q_[0:128, q0 + c0:q0 + W], start=True, stop=not diag,
                                         r=[BK, BQ], w=[B_s], signal=not diag)
                                    if diag:
                                        P.mm(s_ps[:, 0:128], cbs("ident"), cbs("nmf" if fox else "nms"),
                                             start=False, stop=True, r=[B_c], w=[B_s])

                                def fox_exp(kb=kb, Wp=Wp, s_ps=s_ps, B_s=B_s, pi=pi, h=h):
                                    P.act(pT[pi][:, 0:Wp], s_ps[:, 0:Wp], AF.Exp, r=[B_s], w=[B_pT[pi]])

                                def fox_C(kb=kb, c0=c0, W=W, Wp=Wp, pi=pi, h=h, o_ps=o_ps, B_o=B_o,
                                          firstt=firstt, lastt=lastt, q0=q0, nc_here=nc_here):
                                    P.mm(o_ps[0:128, c0:W], vf_all[:, kb, h * 65:h * 65 + 128], pT[pi][:, 0:Wp],
                                         start=firstt, stop=lastt, r=[B_vf[kb], B_pT[pi]],
                                         w=[B_o], signal=lastt)
                                    if lastt:
                                        head_norm_pre(o_ps, B_o, W, True)

                                def sb_e(Wp=Wp, s_ps=s_ps, B_s=B_s, ei=ei):
                                    P.act(ee[ei][:, 0:Wp], s_ps[:, 0:Wp], AF.Exp, r=[B_s], w=[B_ee[ei]])

                                def sb_sp(Wp=Wp, ei=ei, si=si):
                                    P.act(spb[si][:, 0:Wp], ee[ei][:, 0:Wp], AF.Ln, r=[B_ee[ei]], w=[B_spb[si]],
                                          bias=1.0)

                                def sb_B(K_=K_, Q_=Q_, BK=BK, BQ=BQ, kb=kb, q0=q0, c0=c0, W=W, Wp=Wp, diag=diag,
                                         si=si, firstt=firstt, lastt=lastt, sa=sa, B_sa=B_sa, nq=nq, B_nq=B_nq,
                                         d_ps=d_ps, B_d=B_d, san=san, B_san=B_san):
                                    if firstt:
                                        P.v("pool", "memset", sa[:], 0.0, w=[B_sa])
                                        P.v("pool", "memset", san[:], 0.0, w=[B_san])
                                    P.mm(d_ps[:, 0:Wp], cbs("ntri_ge"), spb[si][:, 0:Wp], start=True, stop=False,
                                         r=[B_c, B_spb[si]], w=[B_d], signal=False)
                                    if not firstt:
                                        P.mm(d_ps[:, 0:Wp], cbs("nones"), sa[:, c0:W], start=False, stop=False,
                                             r=[B_c, B_sa], w=[B_d], signal=False)
                                    if diag:
                                        P.mm(d_ps[:, 0:128], cbs("ident"), cbs("nms"), start=False, stop=False,
                                             r=[B_c], w=[B_d], signal=False)
                                    P.mm(d_ps[:, 0:Wp], K_[0:128, kb * 128:(kb + 1) * 128],
                                         Q_[0:128, q0 + c0:q0 + W],
                                         start=False, stop=True, r=[BK, BQ], w=[B_d])
                                    if not lastt:
                                        P.v("dve", "tensor_tensor", san[:, c0:W], sa[:, c0:W], spb[si][:, 0:Wp],
                                            ALU.add, r=[B_spb[si], B_sa], w=[B_san])

                                def sb_x(Wp=Wp, pi=pi, d_ps=d_ps, B_d=B_d):
                                    P.act(pT[pi][:, 0:Wp], d_ps[:, 0:Wp], AF.Exp, r=[B_d], w=[B_pT[pi]])

                                def sb_C(kb=kb, c0=c0, W=W, Wp=Wp, pi=pi, h=h, o_ps=o_ps, B_o=B_o,
                                         firstt=firstt, lastt=lastt, q0=q0, nc_here=nc_here):
                                    if firstt:
                                        P.mm(o_ps[0:128, 0:W], zero_bf[:, 0:128], zero_bf[:, 0:W], start=True,
                                             stop=False, r=[B_misc], w=[B_o], signal=False, skip=True)
                                    w0_ = min(h * 64, 256)
                                    P.mm(o_ps[0:128, c0:W], vs_all[:, kb, w0_:w0_ + 128], pT[pi][:, 0:Wp],
                                         start=False, stop=lastt, r=[B_vs[kb], B_pT[pi]], w=[B_o],
                                         signal=lastt, skip=True)
                                    if lastt:
                                        head_norm_pre(o_ps, B_o, W, False, ro=h * 64 - min(h * 64, 256))

                                stg = [t_A, fox_exp, fox_C] if fox else [t_A, sb_e, sb_sp, sb_B, sb_x, sb_C]
                                if lastt:
                                    def hn_post(W=W, h=h, q0=q0, fox=fox):
                                        if fox:
                                            head_norm_post(W, fg, h, PW + h * 64, q0, True)
                                        else:
                                            head_norm_post(W, sg, h, PW + NH * 64 + h * 64, q0, False,
                                                           ro=h * 64 - min(h * 64, 256))
                                    stg = stg + [(lambda: None)] * (HN_SKEW - 1) + [hn_post]
                                tiles.append(stg)
                    inflight = []
                    ti = 0
                    while ti < len(tiles) or inflight:
                        if ti < len(tiles):
                            inflight.append([tiles[ti], 0])
                            ti += 1
                            emit_list = list(reversed(inflight))
                        else:
                            emit_list = list(reversed(inflight))
                        for ent in emit_list:
                            ent[0][ent[1]]()
                            ent[1] += 1
                        inflight = [e_ for e_ in inflight if e_[1] < len(e_[0])]
                P.barrier()

            fstack = ExitStack()
            wg = sb(fstack, "wg", [128, 8, DFF], BF16)
            wu = sb(fstack, "wu", [128, 8, DFF], BF16)
            B_wg, B_wu = Buf("wg"), Buf("wu")

            def _ffn_w_gen(layer=layer, wg=wg, wu=wu, B_wg=B_wg, B_wu=B_wu):
                yield from load_weight_gen(wg, B_wg, w_gate[layer], 8, DFF, pool_only=True)
                yield from load_weight_gen(wu, B_wu, w_up[layer], 8, DFF, pool_only=True)
            ffn_gen = _ffn_w_gen()
            with ExitStack() as st:
                wo = sb(st, "wo", [128, 8, D], BF16)
                B_wo = Buf("wo")
                load_weight(wo, B_wo, w_out[layer], 8, D)
                g2 = sb(st, "g2", [128, D], F32)
                B_g2 = Buf("g2")
                P.dma(g2[:], norm2_b[layer], "c1", w=[B_g2])
                yT = [sb(st, f"yT{i}", [128, 8, 512], BF16) for i in range(2)]
                B_yT = [Buf(f"yT{i}") for i in range(2)]
                xt = [sb(st, f"xt{i}", [128, D], F32) for i in range(2)]
                B_xt = [Buf(f"xt{i}") for i in range(2)]
                h1 = [sb(st, f"h1{i}", [128, D], F32) for i in range(2)]
                B_h1 = [Buf(f"h1{i}") for i in range(2)]
                junk = sb(st, "junk", [128, D], BF16)
                B_junk = Buf("junk")
                ss = [sb(st, f"ss{i}", [128, 2], F32) for i in range(2)]
                B_ss = [Buf(f"ss{i}") for i in range(2)]
                abf = [sb(st, f"abf{i}", [128, D], BF16) for i in range(2)]
                B_abf = [Buf(f"abf{i}") for i in range(2)]
                aTg = [sb(st, f"aTg{i}", [128, 8, 512], BF16) for i in range(2)]
                B_aTg = [Buf(f"aTg{i}") for i in range(2)]
                ps_o1 = [ps(st, f"ps_o1{i}", [128, 2, 512]) for i in range(2)]
                B_po1 = [Buf(f"ps_o1{i}") for i in range(2)]
                ps_t = [ps(st, f"ps_t{i}", [128, 8, 128], BF16) for i in range(2)]
                B_pst = [Buf(f"ps_t{i}") for i in range(2)]

                def p3a_loadg(g):
                    i = g % 2
                    g0 = g * 512
                    gw = min(512, L - g0)
                    P.dma(yT[i][:, :, 0:gw], YT[:, g0:g0 + gw].rearrange("(k p) t -> p k t", p=128),
                          f"yT{i}", w=[B_yT[i]])

                def p3a_loadx(t):
                    i = t % 2
                    P.dma(xt[i][:], hsrc[t * 128:(t + 1) * 128, :], f"xt{i}", w=[B_xt[i]])

                p3a_loadg(0)
                p3a_loadx(0)

                def p3a_front(t):
                    i = t % 2
                    g = t // 4
                    gi = g % 2
                    j = t % 4
                    if j == 0 and g + 1 < NG:
                        p3a_loadg(g + 1)
                    if t + 1 < NT:
                        p3a_loadx(t + 1)
                    o1, B_o1 = ps_o1[i], B_po1[i]
                    for half in range(2):
                        for k in range(8):
                            P.mm(o1[:, half, :], yT[gi][:, k, j * 128:(j + 1) * 128],
                                 wo[:, k, half * 512:(half + 1) * 512], start=(k == 0), stop=(k == 7),
                                 r=[B_yT[gi], B_wo], w=[B_o1], signal=(k == 7 and half == 1))
                    P.v("dve", "tensor_tensor", h1[i][:], o1[:].rearrange("p a b -> p (a b)"), xt[i][:], ALU.add,
                        r=[B_o1, B_xt[i]], w=[B_h1[i]])
                    P.dma(Hd[t * 128:(t + 1) * 128, :], h1[i][:], f"h1{i}", r=[B_h1[i]], q="pool")
                    P.act(junk[:], h1[i][:], AF.Square, r=[B_h1[i]], w=[B_junk, B_ss[i]], accum=ss[i][:, 0:1])
                    P.act(ss[i][:, 1:2], ss[i][:, 0:1], AF.Ln, r=[B_ss[i]], w=[B_ss[i]], scale=1.0 / D, bias=epsc[:, 0:1])
                    P.act(ss[i][:, 1:2], ss[i][:, 1:2], AF.Exp, r=[B_ss[i]], w=[B_ss[i]], scale=-0.5)
                    P.v("dve", "scalar_tensor_tensor", abf[i][:], h1[i][:], ss[i][:, 1:2], g2[:],
                        ALU.mult, ALU.mult, r=[B_h1[i], B_ss[i], B_g2], w=[B_abf[i]])

                def p3a_back(t):
                    i = t % 2
                    g = t // 4
                    gi = g % 2
                    j = t % 4
                    for k in range(8):
                        P.tr(ps_t[i][:, k, :], abf[i][:, k * 128:(k + 1) * 128], cbs("ident"),
                             r=[B_abf[i], B_c], w=[B_pst[i]], signal=(k == 7))
                    P.act(aTg[gi][:, :, j * 128:(j + 1) * 128], ps_t[i][:], AF.Copy,
                          r=[B_pst[i]], w=[B_aTg[gi]])
                    if j == 3 or t == NT - 1:
                        gw = (j + 1) * 128
                        g0 = g * 512
                        P.dma(A2T[:, g0:g0 + gw].rearrange("(k p) t -> p k t", p=128), aTg[gi][:, :, 0:gw],
                              f"aTg{gi}", r=[B_aTg[gi]], q="pool")

                p3a_front(0)
                for t in range(NT):
                    if t + 1 < NT:
                        p3a_front(t + 1)
                    p3a_back(t)
                    next(ffn_gen, None)
                    next(ffn_gen, None)
                for _ in ffn_gen:
                    pass
            P.barrier()

            with ExitStack() as st:
                wd = sb(st, "wd", [128, 22, D], BF16)
                B_wd = Buf("wd")
                wd_gen = load_weight_gen(wd, B_wd, w_down[layer], 22, D, pool_only=True)
                a2 = [sb(st, f"a2{i}", [128, 8, 512], BF16) for i in range(2)]
                B_a2 = [Buf(f"a2{i}") for i in range(2)]
                gT = sb(st, "gT", [128, 22, 512], BF16)
                B_gT = [Buf(f"gT{f}") for f in range(22)]
                sgl = [sb(st, f"sgl{i}", [128, 512], BF16) for i in range(2)]
                B_sgl = [Buf(f"sgl{i}") for i in range(2)]
                xt = [sb(st, f"xt{i}", [128, D], F32) for i in range(2)]
                B_xt = [Buf(f"xt{i}") for i in range(2)]
                h2 = xt
                B_h2 = B_xt
                ps_g = [ps(st, f"ps_g{i}", [128, 512]) for i in range(2)]
                B_pg = [Buf(f"ps_g{i}") for i in range(2)]
                ps_u = [ps(st, f"ps_u{i}", [128, 512]) for i in range(2)]
                B_pu = [Buf(f"ps_u{i}") for i in range(2)]
                ps_o2 = [ps(st, f"ps_o2{i}", [128, 2, 512]) for i in range(2)]
                B_po2 = [Buf(f"ps_o2{i}") for i in range(2)]

                def p3b_loadg(g):
                    i = g % 2
                    g0 = g * 512
                    gw = min(512, L - g0)
                    P.dma(a2[i][:, :, 0:gw], A2T[:, g0:g0 + gw].rearrange("(k p) t -> p k t", p=128),
                          f"a2{i}", w=[B_a2[i]])

                def p3b_loadx(t):
                    i = t % 2
                    P.dma(xt[i][:], Hd[t * 128:(t + 1) * 128, :], f"xt{i}", w=[B_xt[i]])

                p3b_loadg(0)
                p3b_loadx(0)
                fc = 0
                for g in range(NG):
                    gi = g % 2
                    g0 = g * 512
                    gw = min(512, L - g0)
                    if g + 1 < NG:
                        p3b_loadg(g + 1)
                    for f in range(22):
                        fi = fc % 2
                        fc += 1
                        for k in range(8):
                            P.mm(ps_g[fi][:, 0:gw], wg[:, k, f * 128:(f + 1) * 128], a2[gi][:, k, 0:gw],
                                 start=(k == 0), stop=(k == 7), r=[B_wg, B_a2[gi]], w=[B_pg[fi]], signal=(k == 7))
                        for k in range(8):
                            P.mm(ps_u[fi][:, 0:gw], wu[:, k, f * 128:(f + 1) * 128], a2[gi][:, k, 0:gw],
                                 start=(k == 0), stop=(k == 7), r=[B_wu, B_a2[gi]], w=[B_pu[fi]], signal=(k == 7))
                        P.act(sgl[fi][:, 0:gw], ps_g[fi][:, 0:gw], AF.Silu, r=[B_pg[fi]], w=[B_sgl[fi]])
                        P.v("dve", "tensor_tensor", gT[:, f, 0:gw], sgl[fi][:, 0:gw], ps_u[fi][:, 0:gw], ALU.mult,
                            r=[B_sgl[fi], B_pu[fi]], w=[B_gT[f]])
                        if g == 0:
                            next(wd_gen, None)
                            next(wd_gen, None)
                    if g == 0:
                        for _ in wd_gen:
                            pass
                    for j in range(gw // 128):
                        t = g * 4 + j
                        i = t % 2
                        if t + 1 < NT:
                            p3b_loadx(t + 1)
                        o2, B_o2 = ps_o2[i], B_po2[i]
                        for half in range(2):
                            for f in range(22):
                                P.mm(o2[:, half, :], gT[:, f, j * 128:(j + 1) * 128],
                                     wd[:, f, half * 512:(half + 1) * 512], start=(f == 0), stop=(f == 21),
                                     r=[B_gT[f], B_wd], w=[B_o2], signal=(f == 21 and half == 1))
                        P.v("dve", "tensor_tensor", h2[i][:], o2[:].rearrange("p a b -> p (a b)"), xt[i][:],
                            ALU.add, r=[B_o2], w=[B_h2[i]])
                        P.dma(Hd[t * 128:(t + 1) * 128, :], h2[i][:], f"h2{i}", r=[B_h2[i]], q="pool")
            P.barrier()
            fstack.close()

        with ExitStack() as st:
            gfin = sb(st, "gfin", [128, D], F32)
            B_gf = Buf("gfin")
            P.dma(gfin[:], final_b[:, :], "c1", w=[B_gf])
            junk = sb(st, "junk", [128, D], BF16)
            B_junk = Buf("junk")
            NB4 = 4
            xt = [sb(st, f"xt{i}", [128, D], F32) for i in range(NB4)]
            B_xt = [Buf(f"xt{i}") for i in range(NB4)]
            ss = [sb(st, f"ss{i}", [128, 2], F32) for i in range(NB4)]
            B_ss = [Buf(f"ss{i}") for i in range(NB4)]

            def p4_load(t):
                i = t % NB4
                P.dma(xt[i][:], Hd[t * 128:(t + 1) * 128, :], f"xt{i}", w=[B_xt[i]])

            for t in range(min(2, NT)):
                p4_load(t)
            for t in range(NT):
                i = t % NB4
                if t + 2 < NT:
                    p4_load(t + 2)
                P.act(junk[:], xt[i][:], AF.Square, r=[B_xt[i]], w=[B_junk, B_ss[i]], accum=ss[i][:, 0:1])
                P.act(ss[i][:, 1:2], ss[i][:, 0:1], AF.Ln, r=[B_ss[i]], w=[B_ss[i]], scale=1.0 / D, bias=epsc[:, 0:1])
                P.act(ss[i][:, 1:2], ss[i][:, 1:2], AF.Exp, r=[B_ss[i]], w=[B_ss[i]], scale=-0.5)
                P.v("dve", "scalar_tensor_tensor", xt[i][:], xt[i][:], ss[i][:, 1:2], gfin[:],
                    ALU.mult, ALU.mult, r=[B_ss[i], B_gf], w=[B_xt[i]])
                if L == LFULL:
                    r0 = t * 128 - NMETA
                    lo = max(r0, 0)
                    hi_ = min(r0 + 128, NOUT)
                    if hi_ > lo:
                        P.dma(out[lo:hi_, :], xt[i][lo - r0:hi_ - r0, :], f"ot{i}", r=[B_xt[i]], q="pool")
                else:
                    P.dma(out[t * 128:(t + 1) * 128, :], xt[i][:], f"ot{i}", r=[B_xt[i]], q="pool")
        P.barrier()

        P.emit()
    return nc


def host_inputs(xb, meta_tokens, norm1, w_in, forget_bias, pool_w, pool_scale, fox_out_gain,
                sb_out_gain, w_out, norm2, w_gate, w_up, w_down, final_norm, L=LFULL):
    f = np.float32
    depth = w_in.shape[0]
    xpad = np.zeros((L, D), f)
    xpad[:NMETA] = meta_tokens
    n = min(L - NMETA, xb.shape[0])
    xpad[NMETA:NMETA + n] = xb[:n]
    m = {
        "xpad": xpad,
        "w_in": np.ascontiguousarray(w_in, f),
        "w_out": np.ascontiguousarray(w_out, f),
        "w_gate": np.ascontiguousarray(w_gate, f),
        "w_up": np.ascontiguousarray(w_up, f),
        "w_down": np.ascontiguousarray(w_down, f),
        "pool_w": np.ascontiguousarray(np.transpose(pool_w, (0, 2, 1, 3)), f),
        "norm1_b": np.ascontiguousarray(np.broadcast_to(norm1[:, None, :], (depth, 128, D)), f),
        "norm2_b": np.ascontiguousarray(np.broadcast_to(norm2[:, None, :], (depth, 128, D)), f),
        "final_b": np.ascontiguousarray(np.broadcast_to(final_norm[None, :], (128, D)), f),
        "fb_b": np.ascontiguousarray(np.broadcast_to(forget_bias[:, None, :], (depth, 128, NH)), f),
        "pscale_c": np.ascontiguousarray(np.transpose(pool_scale.reshape(depth, 4, 64), (0, 2, 1)), f),
        "fgain_c": np.ascontiguousarray(np.transpose(fox_out_gain.reshape(depth, NH, 64), (0, 2, 1)), f),
        "sgain_c": np.ascontiguousarray(np.transpose(sb_out_gain.reshape(depth, NH, 64), (0, 2, 1)), f),
        "consts": make_consts(),
    }
    return m


_NC_CACHE = {}


def kernel(x, meta_tokens, norm1, w_in, forget_bias, pool_w, pool_scale, fox_out_gain,
           sb_out_gain, w_out, norm2, w_gate, w_up, w_down, final_norm):
    args = [np.asarray(a) for a in (meta_tokens, norm1, w_in, forget_bias, pool_w, pool_scale,
                                    fox_out_gain, sb_out_gain, w_out, norm2, w_gate, w_up, w_down,
                                    final_norm)]
    x = np.asarray(x)
    if "nc" not in _NC_CACHE:
        _NC_CACHE["nc"] = build_program()
    nc = _NC_CACHE["nc"]
    in_maps = []
    for c in range(8):
        b = c % BATCH
        in_maps.append(host_inputs(x[b], *args))
    res = run_bass_kernel_spmd(nc, in_maps, core_ids=list(range(8)))
    outs = [res.results[b]["out"] for b in range(BATCH)]
    return np.stack(outs, axis=0).astype(np.float32)
```

```python
import math
from contextlib import ExitStack

import numpy as np
import concourse.bass as bass
import concourse.mybir as mybir
from concourse.bass_utils import run_bass_kernel_spmd

F32 = mybir.dt.float32
BF16 = mybir.dt.bfloat16
AF = mybir.ActivationFunctionType
ALU = mybir.AluOpType

D = 1024
NH = 6
HD = 64
PW = 256
DFF = 2816
INW = 2566
NMETA = 16
SEQ = 8192
BATCH = 4
DEPTH = 2
EPS = 1e-6
LFULL = 8320
NEG = -30000.0

C_U, C_FQ, C_FK, C_FV, C_FL, C_SQ, C_SK, C_SV = 0, 256, 640, 1024, 1408, 1414, 1798, 2182


EMBED_WAIT = True
STRICT_SAME_ENGINE = True


class Buf:
    __slots__ = ("name", "w", "r")

    def __init__(self, name):
        self.name = name
        self.w = None
        self.r = {}


class Prog:
    ENG = ("pe", "act", "dve", "pool", "sp")

    def __init__(self, nc, stack):
        self.nc = nc
        self.stack = stack
        self.ops = {e: [] for e in self.ENG}
        self.sem = {}
        self.cnt = {}
        self.waited = {e: {} for e in self.ENG}
        for e in self.ENG:
            self._mk(e)

    def _mk(self, key):
        self.sem[key] = self.stack.enter_context(self.nc.semaphore("s_" + key))
        self.cnt[key] = 0

    def _need(self, eng, tok, need):
        if tok is None:
            return
        key, val = tok
        if key == eng and eng == "pe":
            return
        if key == eng and self.cnt[eng] - val >= 2 and not STRICT_SAME_ENGINE:
            return
        if self.waited[eng].get(key, 0) >= val:
            return
        if need.get(key, 0) < val:
            need[key] = val

    def op(self, eng, fn, reads=(), writes=(), dma=None, signal=True):
        need = {}
        for b in reads:
            self._need(eng, b.w, need)
        for b in writes:
            self._need(eng, b.w, need)
            for k, v in b.r.items():
                if k == eng and not (STRICT_SAME_ENGINE and eng != "pe"):
                    continue
                self._need(eng, (k, v), need)
        for key, val in need.items():
            assert val <= self.cnt[key], f"wait on unsignaled token {key} {val} > {self.cnt[key]}"
            self.ops[eng].append(("w", key, val))
            self.waited[eng][key] = val
        if dma is not None:
            if dma not in self.sem:
                self._mk(dma)
            self.cnt[dma] += 16
            tok = (dma, self.cnt[dma])
            self.ops[eng].append(("i", fn, dma, 16))
        elif signal:
            self.cnt[eng] += 1
            tok = (eng, self.cnt[eng])
            self.ops[eng].append(("i", fn, eng, 1))
        else:
            tok = (eng, self.cnt[eng] + 1)
            self.ops[eng].append(("i", fn, None, 0))
        for b in reads:
            if b.r.get(tok[0], 0) < tok[1]:
                b.r[tok[0]] = tok[1]
        for b in writes:
            b.w = tok
            b.r = {}

    def barrier(self):
        for e in self.ENG:
            for key, val in self.cnt.items():
                if val > 0 and self.waited[e].get(key, 0) < val and key != e:
                    self.ops[e].append(("w", key, val))
                    self.waited[e][key] = val

    def emit(self):
        nc = self.nc
        sem = self.sem

        def replay(name, e):
            pend = []
            for it in self.ops[name]:
                if it[0] == "w":
                    pend.append(it)
                else:
                    emb = None
                    if EMBED_WAIT and pend and name != "sp" and it[3] != 16:
                        emb = pend.pop()
                    for w_ in pend:
                        e.wait_ge(sem[w_[1]], w_[2])
                    pend = []
                    ins = it[1](e)
                    if emb is not None:
                        ins._wait_ge(sem[emb[1]], emb[2])
                    if it[2] is not None:
                        ins.then_inc(sem[it[2]], it[3])
            for w_ in pend:
                e.wait_ge(sem[w_[1]], w_[2])

        with nc.Block() as block:
            @block.tensor
            def _(e):
                replay("pe", e)

            @block.scalar
            def _(e):
                replay("act", e)

            @block.vector
            def _(e):
                replay("dve", e)

            @block.gpsimd
            def _(e):
                replay("pool", e)

            @block.sync
            def _(e):
                replay("sp", e)

    def mm(self, out, lhsT, rhs, start=True, stop=True, r=(), w=(), signal=True, skip=False):
        def fn(e):
            return e.matmul(out, lhsT, rhs, start=start, stop=stop, skip_group_check=skip)
        self.op("pe", fn, r, w, signal=signal)

    def tr(self, out, in_, ident, r=(), w=(), signal=True):
        self.op("pe", lambda e: e.transpose(out, in_, ident), r, w, signal=signal)

    def act(self, out, in_, func, r=(), w=(), bias=None, scale=None, accum=None):
        kw = {}
        if bias is not None:
            kw["bias"] = bias
        if scale is not None:
            kw["scale"] = scale
        if accum is not None:
            kw["accum_out"] = accum
        self.op("act", lambda e: e.activation(out, in_, func, **kw), r, w)

    def v(self, eng, name, *args, r=(), w=(), **kw):
        self.op(eng, lambda e: getattr(e, name)(*args, **kw), r, w)

    def dma(self, out, in_, sem, r=(), w=(), q="sp"):
        self.op(q, lambda e: e.dma_start(out=out, in_=in_), r, w, dma=sem)


def _const_layout():
    names = ["ident", "tri_le", "ones", "nmf", "nms", "pms", "tri_ge", "ntri_ge", "nones"]
    for g in range(4):
        names += [f"pd{g}", f"pp{g}", f"pz{g}"]
    off = {}
    o = 0
    for n in names:
        off[n] = o
        o += 128
    return off, o


def make_consts():
    off, n = _const_layout()
    c = np.zeros((128, n), np.float32)
    j = np.arange(128)[:, None]
    i = np.arange(128)[None, :]
    c[:, off["ident"]:off["ident"] + 128] = (j == i)
    c[:, off["tri_le"]:off["tri_le"] + 128] = (j <= i)
    c[:, off["ones"]:off["ones"] + 128] = 1.0
    c[:, off["nmf"]:off["nmf"] + 128] = np.where(j > i, NEG, 0.0)
    c[:, off["nms"]:off["nms"] + 128] = np.where(j >= i, NEG, 0.0)
    c[:, off["pms"]:off["pms"] + 128] = np.where(j >= i, -NEG, 0.0)
    c[:, off["tri_ge"]:off["tri_ge"] + 128] = (j >= i)
    c[:, off["ntri_ge"]:off["ntri_ge"] + 128] = -1.0 * (j >= i)
    c[:, off["nones"]:off["nones"] + 128] = -1.0
    for g, wnd in enumerate((2, 4, 8, 16)):
        pd = np.where((j <= i) & (j > i - wnd), 1.0 / wnd, 0.0) - (j == i)
        pp = np.where((j - 128 > i - wnd), 1.0 / wnd, 0.0)
        cnt = np.minimum(i + 1, wnd).astype(np.float64)
        pz = np.where((j <= i) & (j > i - wnd), 1.0 / cnt, 0.0) - (j == i)
        c[:, off[f"pd{g}"]:off[f"pd{g}"] + 128] = pd
        c[:, off[f"pp{g}"]:off[f"pp{g}"] + 128] = pp
        c[:, off[f"pz{g}"]:off[f"pz{g}"] + 128] = pz
    return c


def build_program(L=LFULL, depth=DEPTH, dbg=False):
    assert L % 128 == 0
    NT = L // 128
    NG = (L + 511) // 512
    coff, ncon = _const_layout()

    nc = bass.Bass("TRN2", target_bir_lowering=False)
    kind_dbg = "ExternalOutput" if dbg else "Internal"

    def din(name, shape, dt=F32):
        return nc.dram_tensor(name, list(shape), dt, kind="ExternalInput").ap()

    def dscr(name, shape, dt):
        return nc.dram_tensor(name, list(shape), dt, kind=kind_dbg).ap()

    xpad = din("xpad", [L, D])
    w_in = din("w_in", [depth, D, INW])
    w_out = din("w_out", [depth, D, D])
    w_gate = din("w_gate", [depth, D, DFF])
    w_up = din("w_up", [depth, D, DFF])
    w_down = din("w_down", [depth, DFF, D])
    pool_w = din("pool_w", [depth, 64, 4, 64])
    norm1_b = din("norm1_b", [depth, 128, D])
    norm2_b = din("norm2_b", [depth, 128, D])
    final_b = din("final_b", [128, D])
    fb_b = din("fb_b", [depth, 128, NH])
    pscale_c = din("pscale_c", [depth, 64, 4])
    fgain_c = din("fgain_c", [depth, 64, NH])
    sgain_c = din("sgain_c", [depth, 64, NH])
    consts = din("consts", [128, ncon])
    out = nc.dram_tensor("out", [SEQ if L == LFULL else L, D], F32, kind="ExternalOutput").ap()
    NOUT = out.shape[0]

    Hd = dscr("Hd", [L, D], F32)
    QK = dscr("QK", [4, 384, L], BF16)
    QFc = dscr("QFc", [NH, L], BF16)
    YT = dscr("YT", [D, L], BF16)
    A2T = dscr("A2T", [D, L], BF16)
    NCK = dscr("NCK", [NH, 3, L], BF16)

    with ExitStack() as gstack:
        P = Prog(nc, gstack)

        uid = [0]

        def sb(stack, name, shape, dt):
            uid[0] += 1
            return stack.enter_context(nc.sbuf_tensor(f"{name}_{uid[0]}", list(shape), dt))

        def ps(stack, name, shape, dt=F32):
            uid[0] += 1
            return stack.enter_context(nc.psum_tensor(f"{name}_{uid[0]}", list(shape), dt))

        cf = sb(gstack, "cf", [128, 384], F32)
        cb = sb(gstack, "cb", [128, ncon], BF16)
        WCH = 512
        wst = [sb(gstack, f"wst{i}", [128, WCH], F32) for i in range(2)]
        B_wst = [Buf(f"wst{i}") for i in range(2)]
        B_c = Buf("consts")
        zero_bf = sb(gstack, "zero_bf", [128, 512], BF16)
        onesrow = sb(gstack, "onesrow", [128, 128], F32)
        ones64 = sb(gstack, "ones64", [128, 128], F32)
        B_misc = Buf("misc")

        P.dma(cf[:], consts[:, 0:384], "c0", w=[B_c])
        for c0_ in range(0, ncon, WCH):
            cw_ = min(WCH, ncon - c0_)
            i_ = (c0_ // WCH) % 2
            P.dma(wst[i_][:, 0:cw_], consts[:, c0_:c0_ + cw_], f"wst{i_}", w=[B_wst[i_]])
            P.v("dve", "tensor_copy", cb[:, c0_:c0_ + cw_], wst[i_][:, 0:cw_], r=[B_wst[i_]], w=[B_c])
        P.v("pool", "memset", zero_bf[:], 0.0, w=[B_misc])
        P.v("pool", "memset", onesrow[:], 1.0, w=[B_misc])
        P.v("pool", "memset", ones64[:], 1.0 / 64.0, w=[B_misc])
        fxn = sb(gstack, "fxn", [128, 128], F32)
        P.v("pool", "memset", fxn[:], 1.0 / 64.0, w=[B_misc])
        P.v("pool", "memset", fxn[64:65, :], 1.0, w=[B_misc])
        epsc = sb(gstack, "epsc", [128, 1], F32)
        P.v("pool", "memset", epsc[:], EPS, w=[B_misc])

        def cfs(name, rows=128, cols=128):
            o = coff[name]
            return cf[0:rows, o:o + cols]

        def cbs(name, rows=128, cols=128):
            o = coff[name]
            return cb[0:rows, o:o + cols]

        wcount = [0]

        def load_weight_gen(dst, B_dst, src, kchunks, ncols, col0=0, dst_col0=0, pool_only=False):
            for k in range(kchunks):
                for cc in range(0, ncols, WCH):
                    cw = min(WCH, ncols - cc)
                    i = wcount[0] % 2
                    wcount[0] += 1
                    P.dma(wst[i][:, 0:cw], src[k * 128:(k + 1) * 128, col0 + cc:col0 + cc + cw],
                          f"wst{i}", w=[B_wst[i]])
                    eng = "pool" if (pool_only or ((wcount[0] // 2) % 2)) else "dve"
                    P.v(eng, "tensor_copy", dst[:, k, dst_col0 + cc:dst_col0 + cc + cw], wst[i][:, 0:cw],
                        r=[B_wst[i]], w=[B_dst])
                    yield

        def load_weight(dst, B_dst, src, kchunks, ncols, col0=0, dst_col0=0):
            for _ in load_weight_gen(dst, B_dst, src, kchunks, ncols, col0, dst_col0):
                pass

        for layer in range(depth):
            hsrc = xpad if layer == 0 else Hd
            last = layer == depth - 1
            with ExitStack() as astack:
                vf_all = sb(astack, "vf_all", [128, NT, 5 * 65 + 128 + 3], BF16)
                vs_all = sb(astack, "vs_all", [128, NT, NH * 64], BF16)
                negc_all = sb(astack, "negc_all", [128, NT, NH], F32)
                B_vf = [Buf(f"vf{t}") for t in range(NT)]
                B_vs = [Buf(f"vs{t}") for t in range(NT)]
                B_negc = [Buf(f"negc{t}") for t in range(NT)]
                B_vinit = Buf("vinit")
                P.v("pool", "memset", vf_all[:], 1.0, w=[B_vinit])
                for b in B_vf:
                    b.w = B_vinit.w

                with ExitStack() as st:
                    wt = sb(st, "wt", [128, 8, 1030], BF16)
                    wf = sb(st, "wf", [128, 8, 1536], BF16)
                    pw = sb(st, "pw", [64, 4, 64], BF16)
                    pwf = sb(st, "pwf", [64, 4, 64], F32)
                    g1 = sb(st, "g1", [128, D], F32)
                    fbt = sb(st, "fbt", [128, NH], F32)
                    psc = sb(st, "psc", [64, 4], F32)
                    B_wt, B_wf, B_small = Buf("wt"), Buf("wf"), Buf("small1")
                    load_weight(wt, B_wt, w_in[layer], 8, 256, C_U, 0)
                    load_weight(wt, B_wt, w_in[layer], 8, 384, C_FV, 256)
                    load_weight(wt, B_wt, w_in[layer], 8, 384, C_SV, 640)
                    load_weight(wt, B_wt, w_in[layer], 8, 6, C_FL, 1024)
                    load_weight(wf, B_wf, w_in[layer], 8, 768, C_FQ, 0)
                    load_weight(wf, B_wf, w_in[layer], 8, 768, C_SQ, 768)
                    P.dma(g1[:], norm1_b[layer], "c1", w=[B_small])
                    P.dma(fbt[:], fb_b[layer], "c2", w=[B_small])
                    P.dma(psc[:], pscale_c[layer], "c3", w=[B_small])
                    P.dma(pwf[:], pool_w[layer], "c4", w=[B_small])
                    P.v("dve", "tensor_copy", pw[:], pwf[:], r=[B_small], w=[B_small])

                    xt = [sb(st, f"xt{i}", [128, D], F32) for i in range(2)]
                    B_xt = [Buf(f"xt{i}") for i in range(2)]
                    junk = sb(st, "junk", [128, D], BF16)
                    B_junk = Buf("junk")
                    ss = [sb(st, f"ss{i}", [128, 2], F32) for i in range(2)]
                    B_ss = [Buf(f"ss{i}") for i in range(2)]
                    abf = [sb(st, f"abf{i}", [128, D], BF16) for i in range(2)]
                    B_abf = [Buf(f"abf{i}") for i in range(2)]
                    aTg = [sb(st, f"aTg{i}", [128, 8, 512], BF16) for i in range(2)]
                    B_aTg = [Buf(f"aTg{i}") for i in range(2)]
                    ubf = [sb(st, f"ubf{i}", [128, 256], BF16) for i in range(2)]
                    B_ubf = [Buf(f"ubf{i}") for i in range(2)]
                    flt = [sb(st, f"flt{i}", [128, 2 * NH], F32) for i in range(2)]
                    B_flt = [Buf(f"flt{i}") for i in range(2)]
                    carry = [sb(st, f"carry{i}", [128, NH], F32) for i in range(2)]
                    B_carry = [Buf(f"carry{i}") for i in range(2)]
                    cst = [sb(st, f"cst{i}", [NH, 512], BF16) for i in range(2)]
                    B_cst = [Buf(f"cst{i}") for i in range(2)]
                    qst = [sb(st, f"qst{i}", [128, 512], BF16) for i in range(3)]
                    B_qst = [Buf(f"qst{i}") for i in range(3)]
                    dT = [sb(st, f"dT{i}", [64, 4, 128], BF16) for i in range(2)]
                    B_dT = [Buf(f"dT{i}") for i in range(2)]
                    ypl = [sb(st, f"ypl{i}", [64, 4, 128], BF16) for i in range(2)]
                    B_ypl = [Buf(f"ypl{i}") for i in range(2)]
                    ps_t = [ps(st, f"ps_t{i}", [128, 8, 128], BF16) for i in range(1)]
                    B_pst = [Buf(f"ps_t{i}") for i in range(1)]
                    ps_a = [ps(st, f"ps_a{i}", [128, 512]) for i in range(3)]
                    B_psa = [Buf(f"ps_a{i}") for i in range(3)]
                    ps_f = [ps(st, f"ps_f{i}", [128, 512]) for i in range(2)]
                    B_psf = [Buf(f"ps_f{i}") for i in range(2)]
                    ps_m = [ps(st, f"ps_m{i}", [128, 512]) for i in range(2)]
                    B_psm = [Buf(f"ps_m{i}") for i in range(2)]

                    P.v("pool", "memset", carry[0][:], 0.0, w=[B_carry[0]])

                    def p1_load(t):
                        i = t % 2
                        P.dma(xt[i][:], hsrc[t * 128:(t + 1) * 128, :], f"xt{i}", w=[B_xt[i]])

                    p1_load(0)
                    fcount = 0

                    def p1_front(t):
                        i = t % 2
                        gi = (t // 4) % 2
                        j = t % 4
                        if t + 1 < NT:
                            p1_load(t + 1)
                        P.act(junk[:], xt[i][:], AF.Square, r=[B_xt[i]], w=[B_junk, B_ss[i]],
                              accum=ss[i][:, 0:1])
                        P.act(ss[i][:, 1:2], ss[i][:, 0:1], AF.Ln, r=[B_ss[i]], w=[B_ss[i]], scale=1.0 / D, bias=epsc[:, 0:1])
                        P.act(ss[i][:, 1:2], ss[i][:, 1:2], AF.Exp, r=[B_ss[i]], w=[B_ss[i]], scale=-0.5)
                        P.v("dve", "scalar_tensor_tensor", abf[i][:], xt[i][:], ss[i][:, 1:2], g1[:],
                            ALU.mult, ALU.mult, r=[B_xt[i], B_ss[i], B_small], w=[B_abf[i]])
                        for k in range(8):
                            P.tr(ps_t[0][:, k, :], abf[i][:, k * 128:(k + 1) * 128], cbs("ident"),
                                 r=[B_abf[i], B_c], w=[B_pst[0]], signal=(k == 7))
                        P.act(aTg[gi][:, :, j * 128:(j + 1) * 128], ps_t[0][:], AF.Copy,
                              r=[B_pst[0]], w=[B_aTg[gi]])

                    def p1_back(t):
                        nonlocal fcount
                        i = t % 2
                        gi = (t // 4) % 2
                        j = t % 4
                        aT = aTg[gi]
                        for k in range(8):
                            P.mm(ps_a[0][:, 0:256], aT[:, k, j * 128:(j + 1) * 128], wt[:, k, 0:256],
                                 start=(k == 0), stop=(k == 7), r=[B_aTg[gi], B_wt], w=[B_psa[0]],
                                 signal=False)
                        for k in range(8):
                            P.mm(ps_a[0][:, 256:262], aT[:, k, j * 128:(j + 1) * 128], wt[:, k, 1024:1030],
                                 start=(k == 0), stop=(k == 7), r=[B_aTg[gi], B_wt], w=[B_psa[0]],
                                 signal=(k == 7), skip=True)
                        for k in range(8):
                            P.mm(ps_a[1][:, 0:384], aT[:, k, j * 128:(j + 1) * 128], wt[:, k, 256:640],
                                 start=(k == 0), stop=(k == 7), r=[B_aTg[gi], B_wt], w=[B_psa[1]],
                                 signal=(k == 7))
                        for k in range(8):
                            P.mm(ps_a[2][:, 0:384], aT[:, k, j * 128:(j + 1) * 128], wt[:, k, 640:1024],
                                 start=(k == 0), stop=(k == 7), r=[B_aTg[gi], B_wt], w=[B_psa[2]],
                                 signal=(k == 7))
                        P.act(ubf[i][:], ps_a[0][:, 0:256], AF.Copy, r=[B_psa[0]], w=[B_ubf[i]])
                        P.v("dve", "tensor_tensor", flt[i][:, 0:NH], ps_a[0][:, 256:262], fbt[:], ALU.add,
                            r=[B_psa[0], B_small], w=[B_flt[i]])
                        P.v("dve", "tensor_copy",
                            vf_all[:, t, 0:NH * 65].rearrange("p (h d) -> p h d", d=65)[:, :, 0:64],
                            ps_a[1][:, 0:384].rearrange("p (h d) -> p h d", d=64),
                            r=[B_psa[1]], w=[B_vf[t]])
                        P.act(vs_all[:, t, :], ps_a[2][:, 0:384], AF.Copy, r=[B_psa[2]], w=[B_vs[t]])
                        P.act(flt[i][:, 0:NH], flt[i][:, 0:NH], AF.Exp, r=[B_flt[i]], w=[B_flt[i]], scale=-1.0)
                        P.act(flt[i][:, NH:2 * NH], flt[i][:, 0:NH], AF.Ln, r=[B_flt[i]], w=[B_flt[i]], bias=1.0)
                        m = ps_m[t % 2]
                        B_m = B_psm[t % 2]
                        P.mm(m[:, 0:NH], cfs("tri_le"), flt[i][:, NH:2 * NH], r=[B_c, B_flt[i]], w=[B_m],
                             signal=False)
                        P.mm(m[:, NH:2 * NH], cfs("ones"), flt[i][:, NH:2 * NH], r=[B_c, B_flt[i]], w=[B_m],
                             skip=True)
                        ci, co = t % 2, (t + 1) % 2
                        P.v("dve", "tensor_tensor", negc_all[:, t, :], m[:, 0:NH], carry[ci][:], ALU.add,
                            r=[B_m, B_carry[ci]], w=[B_negc[t]])
                        P.v("dve", "tensor_tensor", carry[co][:], m[:, NH:2 * NH], carry[ci][:], ALU.add,
                            r=[B_m, B_carry[ci]], w=[B_carry[co]])
                        P.tr(m[0:NH, 128:256], negc_all[:, t, :], cfs("ident"), r=[B_negc[t], B_c], w=[B_m])
                        P.act(cst[gi][:, j * 128:(j + 1) * 128], m[0:NH, 128:256], AF.Copy,
                              r=[B_m], w=[B_cst[gi]], scale=-1.0)
                        ip = (t + 1) % 2
                        group_end = (j == 3) or (t == NT - 1)
                        pd_ps = ps_f[fcount % 2]
                        B_pd = B_psf[fcount % 2]
                        fcount += 1
                        for g in range(4):
                            if t == 0:
                                P.mm(pd_ps[0:64, g * 128:(g + 1) * 128], ubf[i][:, g * 64:(g + 1) * 64],
                                     cbs("pz%d" % g), r=[B_ubf[i], B_c], w=[B_pd], signal=(g == 3), skip=True)
                            else:
                                P.mm(pd_ps[0:64, g * 128:(g + 1) * 128], ubf[i][:, g * 64:(g + 1) * 64],
                                     cbs("pd%d" % g), start=True, stop=False, r=[B_ubf[i], B_c], w=[B_pd],
                                     signal=False, skip=True)
                                P.mm(pd_ps[0:64, g * 128:(g + 1) * 128], ubf[ip][:, g * 64:(g + 1) * 64],
                                     cbs("pp%d" % g), start=False, stop=True, r=[B_ubf[ip], B_c], w=[B_pd],
                                     signal=(g == 3), skip=True)
                        P.v("dve", "tensor_copy", dT[i][:].rearrange("p g t -> p (g t)"), pd_ps[0:64, 0:512],
                            r=[B_pd], w=[B_dT[i]])
                        py_ps = ps_f[fcount % 2]
                        B_py = B_psf[fcount % 2]
                        fcount += 1
                        for g in range(4):
                            P.mm(py_ps[0:64, g * 128:(g + 1) * 128], pw[:, g, :], dT[i][:, g, :],
                                 r=[B_small, B_dT[i]], w=[B_py], signal=(g == 3), skip=True)
                        for g in range(4):
                            P.act(ypl[i][:, g, :], py_ps[0:64, g * 128:(g + 1) * 128], AF.Identity,
                                  r=[B_py, B_small], w=[B_ypl[i]], scale=psc[:, g:g + 1])
                        P.dma(YT[0:256, t * 128:(t + 1) * 128].rearrange("(g d) t -> d g t", d=64), ypl[i][:],
                              f"ypl{i}", r=[B_ypl[i]], q="pool")
                        if group_end:
                            gw = (j + 1) * 128
                            g0 = (t // 4) * 512
                            P.dma(QFc[:, g0:g0 + gw], cst[gi][:, 0:gw], f"cst{gi}", r=[B_cst[gi]], q="pool")
                            for mt in range(12):
                                pf = ps_f[fcount % 2]
                                B_pf = B_psf[fcount % 2]
                                fcount += 1
                                for k in range(8):
                                    P.mm(pf[:, 0:gw], wf[:, k, mt * 128:(mt + 1) * 128], aT[:, k, 0:gw],
                                         start=(k == 0), stop=(k == 7), r=[B_wf, B_aTg[gi]], w=[B_pf],
                                         signal=(k == 7))
                                kind, mm_ = mt // 3, mt % 3
                                qi = mt % 3
                                scale = 0.125 if kind in (0, 2) else 1.0
                                if mt % 2 == 0:
                                    P.act(qst[qi][:, 0:gw], pf[:, 0:gw], AF.Copy, r=[B_pf], w=[B_qst[qi]],
                                          scale=scale)
                                else:
                                    P.v("dve", "tensor_scalar", qst[qi][:, 0:gw], pf[:, 0:gw], scale, None,
                                        ALU.mult, r=[B_pf], w=[B_qst[qi]])
                                P.dma(QK[kind, mm_ * 128:(mm_ + 1) * 128, g0:g0 + gw], qst[qi][:, 0:gw],
                                      f"qst{qi}", r=[B_qst[qi]], q="pool")
                    p1_front(0)
                    for t in range(NT):
                        if t + 1 < NT:
                            p1_front(t + 1)
                        p1_back(t)
                P.barrier()

                with ExitStack() as st:
                    KB = [sb(st, f"KB{i}", [128, L], BF16) for i in range(2)]
                    QB = [sb(st, f"QB{i}", [128, L], BF16) for i in range(2)]
                    B_KB = [Buf(f"KB{i}") for i in range(2)]
                    B_QB = [Buf(f"QB{i}") for i in range(2)]
                    fg = sb(st, "fg", [128, NH], F32)
                    sg = sb(st, "sg", [128, NH], F32)
                    B_g = Buf("gains")
                    P.dma(fg[0:64, :], fgain_c[layer], "c1", w=[B_g])
                    P.dma(sg[0:64, :], sgain_c[layer], "c2", w=[B_g])
                    P.dma(sg[64:128, :], sgain_c[layer], "c3", w=[B_g])
                    for i in range(2):
                        P.v("pool", "memset", KB[i][:, :], 0.0, w=[B_KB[i]])
                        P.v("pool", "memset", QB[i][:, :], 0.0, w=[B_QB[i]])
                        P.v("pool", "memset", QB[i][64:68, :], 1.0, w=[B_QB[i]])
                        P.v("pool", "memset", KB[i][64:65, :], 1.0, w=[B_KB[i]])
                    pT = [sb(st, f"pT{i}", [128, 512], BF16) for i in range(3)]
                    B_pT = [Buf(f"pT{i}") for i in range(3)]
                    ee = [sb(st, f"ee{i}", [128, 512], F32) for i in range(2)]
                    B_ee = [Buf(f"ee{i}") for i in range(2)]
                    spb = [sb(st, f"spb{i}", [128, 512], BF16) for i in range(3)]
                    B_spb = [Buf(f"spb{i}") for i in range(3)]
                    sacc = [[sb(st, f"sacc{i}_{k}", [128, 512], BF16) for k in range(2)] for i in range(2)]
                    B_sacc = [[Buf(f"sacc{i}_{k}") for k in range(2)] for i in range(2)]
                    negq = [None, None]
                    B_negq = [None, None]
                    osb_ = sb(st, "osb", [128, 512], F32)
                    osb = [osb_, osb_]
                    B_osb_ = Buf("osb")
                    B_osb = [B_osb_, B_osb_]
                    rden = sb(st, "rden", [128, 512], F32)
                    ysq = sb(st, "ysq", [128, 512], F32)
                    rstd = rden
                    B_rden, B_ysq = Buf("rden"), Buf("ysq")
                    B_ysq2 = Buf("ysq2")
                    B_rstd = B_rden
                    yst_ = sb(st, "yst", [128, 512], BF16)
                    yst = [yst_, yst_]
                    B_yst_ = Buf("yst")
                    B_yst = [B_yst_, B_yst_]
                    ps_s = [ps(st, f"ps_s{i}", [128, 512]) for i in range(3)]
                    B_pss = [Buf(f"ps_s{i}") for i in range(3)]
                    ps_d = [ps(st, f"ps_d{i}", [128, 512]) for i in range(2)]
                    B_psd = [Buf(f"ps_d{i}") for i in range(2)]
                    ps_o = [ps(st, f"ps_o{i}", [128, 512]) for i in range(2)]
                    B_pso = [Buf(f"ps_o{i}") for i in range(2)]
                    ps_n_ = ps(st, "ps_n", [128, 512])
                    ps_n = [ps_n_, ps_n_]
                    B_psn_ = Buf("ps_n")
                    B_psn = [B_psn_, B_psn_]

                    B_nck = Buf("nck")

                    def head_load(hh):
                        i = hh % 2
                        if hh < NH:
                            P.tr(ps_n_[0:NT, 0:128], negc_all[:, :, hh], cfs("ident"), r=[B_c], w=[B_psn_])
                            P.act(yst_[0:NT, 0:128], ps_n_[0:NT, 0:128], AF.Copy, r=[B_psn_], w=[B_yst_])
                            P.v("dve", "tensor_tensor", rden[0:NT, 0:128], ps_n_[0:NT, 0:128], yst_[0:NT, 0:128],
                                ALU.subtract, r=[B_psn_, B_yst_], w=[B_rden])
                            P.v("dve", "tensor_copy", yst_[0:NT, 128:256], rden[0:NT, 0:128], r=[B_rden],
                                w=[B_yst_])
                            P.v("dve", "tensor_tensor", yst_[0:NT, 256:384], rden[0:NT, 0:128],
                                yst_[0:NT, 128:256], ALU.subtract, r=[B_rden], w=[B_yst_])
                            for j_ in range(3):
                                P.dma(NCK[hh, j_, :].rearrange("(t p) -> t p", p=128),
                                      yst_[0:NT, j_ * 128:(j_ + 1) * 128], "nck", r=[B_yst_], w=[B_nck])
                            P.dma(KB[i][65:68, :], NCK[hh, :, :], f"KB{i}", r=[B_nck], w=[B_KB[i]])
                            P.dma(QB[i][0:64, :], QK[0, hh * 64:(hh + 1) * 64, :], f"QB{i}", w=[B_QB[i]])
                            P.dma(QB[i][64:65, :], QFc[hh:hh + 1, :], f"QB{i}", w=[B_QB[i]])
                            P.dma(KB[i][0:64, :], QK[1, hh * 64:(hh + 1) * 64, :], f"KB{i}", w=[B_KB[i]])
                        else:
                            h2 = hh - NH
                            if h2 < 2:
                                P.v("pool", "memset", QB[i][64:68, :], 0.0, w=[B_QB[i]])
                            P.dma(QB[i][0:64, :], QK[2, h2 * 64:(h2 + 1) * 64, :], f"QB{i}", w=[B_QB[i]])
                            P.dma(KB[i][0:64, :], QK[3, h2 * 64:(h2 + 1) * 64, :], f"KB{i}", w=[B_KB[i]])

                    HN_SKEW = 3

                    def head_norm_pre(o_ps, B_o, W, fox, ro=0):
                        nrows = 65 if fox else 64
                        R = slice(ro, ro + 64)
                        P.v("dve", "tensor_copy", osb_[ro:ro + nrows, 0:W], o_ps[ro:ro + nrows, 0:W], r=[B_o],
                            w=[B_osb_])
                        oth = 64 - ro
                        P.v("pool", "memset", ysq[oth:oth + 64, 0:W], 0.0, w=[B_ysq2])
                        P.v("pool", "tensor_tensor", ysq[R, 0:W], osb_[R, 0:W], osb_[R, 0:W], ALU.mult,
                            r=[B_osb_], w=[B_ysq])
                        if fox:
                            P.v("dve", "scalar_tensor_tensor", ysq[64:65, 0:W], osb_[64:65, 0:W], EPS,
                                osb_[64:65, 0:W], ALU.mult, ALU.mult, r=[B_osb_], w=[B_ysq2])

                    def head_norm_post(W, gains, h, yrow0, q0, fox, ro=0):
                        R = slice(ro, ro + 64)
                        n1 = ps_n_
                        if fox:
                            P.mm(n1[:, 0:W], fxn[0:128, 0:128], ysq[0:128, 0:W], r=[B_misc, B_ysq, B_ysq2],
                                 w=[B_psn_])
                            P.act(rstd[R, 0:W], n1[R, 0:W], AF.Ln, r=[B_psn_], w=[B_rstd])
                        else:
                            P.mm(n1[:, 0:W], ones64[0:128, 0:128], ysq[0:128, 0:W], r=[B_misc, B_ysq, B_ysq2],
                                 w=[B_psn_])
                            P.act(rstd[R, 0:W], n1[R, 0:W], AF.Ln, r=[B_psn_], w=[B_rstd], bias=epsc[R, 0:1])
                        P.act(rstd[R, 0:W], rstd[R, 0:W], AF.Exp, r=[B_rstd], w=[B_rstd], scale=-0.5)
                        P.v("dve", "scalar_tensor_tensor", yst_[R, 0:W], osb_[R, 0:W], gains[R, h:h + 1],
                            rstd[R, 0:W], ALU.mult, ALU.mult, r=[B_osb_, B_rstd, B_g], w=[B_yst_])
                        P.dma(YT[yrow0:yrow0 + 64, q0:q0 + W], yst_[R, 0:W], "yst0", r=[B_yst_], q="pool")

                    head_load(0)
                    tiles = []
                    tcount = 0
                    ncount = 0
                    for hh in range(2 * NH):
                        hi = hh % 2
                        K_, Q_ = KB[hi], QB[hi]
                        BK, BQ = B_KB[hi], B_QB[hi]
                        fox = hh < NH
                        h = hh if fox else hh - NH
                        head_tile_idx = 0
                        for qb in range(NG):
                            q0 = qb * 512
                            W = min(512, L - q0)
                            nkb = (q0 + W) // 128
                            gq = (hh * NG + qb) % 2
                            o_ps = ps_o[gq]
                            B_o = B_pso[gq]
                            sa2 = sacc[gq]
                            B_sa2 = B_sacc[gq]
                            nq = negq[gq]
                            B_nq = B_negq[gq]
                            order = list(range(nkb)) if fox else list(range(nkb - 1, -1, -1))
                            for oi_, kb in enumerate(order):
                                c0 = max(0, kb * 128 - q0)
                                Wp = W - c0
                                diag = kb * 128 >= q0
                                firstt = oi_ == 0
                                lastt = oi_ == nkb - 1
                                sa, B_sa = sa2[oi_ % 2], B_sa2[oi_ % 2]
                                san, B_san = sa2[(oi_ + 1) % 2], B_sa2[(oi_ + 1) % 2]
                                si = tcount % 3
                                ei = tcount % 2
                                pi = tcount % 3
                                tcount += 1
                                s_ps, B_s = ps_s[si], B_pss[si]
                                d_ps, B_d = ps_d[ei], B_psd[ei]
                                pre_load = (hh + 1) if (head_tile_idx == 8 and hh + 1 < 2 * NH) else None
                                head_tile_idx += 1
                                nc_here = ncount
                                if lastt:
                                    ncount += 1

                                def t_A(K_=K_, Q_=Q_, BK=BK, BQ=BQ, kb=kb, q0=q0, c0=c0, W=W, Wp=Wp,
                                        diag=diag, s_ps=s_ps, B_s=B_s, pre_load=pre_load, fox=fox):
                                    if pre_load is not None:
                                        head_load(pre_load)
                                    P.mm(s_ps[:, 0:Wp], K_[0:128, kb * 128:(kb + 1) * 128],
                                         Q_[0:128, q0 + c0:q0 + W], start=True, stop=not diag,
                                         r=[BK, BQ], w=[B_s], signal=not diag)
                                    if diag:
                                        P.mm(s_ps[:, 0:128], cbs("ident"), cbs("nmf" if fox else "nms"),
                                             start=False, stop=True, r=[B_c], w=[B_s])

                                def fox_exp(kb=kb, Wp=Wp, s_ps=s_ps, B_s=B_s, pi=pi, h=h):
                                    P.act(pT[pi][:, 0:Wp], s_ps[:, 0:Wp], AF.Exp, r=[B_s], w=[B_pT[pi]])

                                def fox_C(kb=kb, c0=c0, W=W, Wp=Wp, pi=pi, h=h, o_ps=o_ps, B_o=B_o,
                                          firstt=firstt, lastt=lastt, q0=q0, nc_here=nc_here):
                                    P.mm(o_ps[0:128, c0:W], vf_all[:, kb, h * 65:h * 65 + 128], pT[pi][:, 0:Wp],
                                         start=firstt, stop=lastt, r=[B_vf[kb], B_pT[pi]],
                                         w=[B_o], signal=lastt)
                                    if lastt:
                                        head_norm_pre(o_ps, B_o, W, True)

                                def sb_e(Wp=Wp, s_ps=s_ps, B_s=B_s, ei=ei):
                                    P.act(ee[ei][:, 0:Wp], s_ps[:, 0:Wp], AF.Exp, r=[B_s], w=[B_ee[ei]])

                                def sb_sp(Wp=Wp, ei=ei, si=si):
                                    P.act(spb[si][:, 0:Wp], ee[ei][:, 0:Wp], AF.Ln, r=[B_ee[ei]], w=[B_spb[si]],
                                          bias=1.0)

                                def sb_B(K_=K_, Q_=Q_, BK=BK, BQ=BQ, kb=kb, q0=q0, c0=c0, W=W, Wp=Wp, diag=diag,
                                         si=si, firstt=firstt, lastt=lastt, sa=sa, B_sa=B_sa, nq=nq, B_nq=B_nq,
                                         d_ps=d_ps, B_d=B_d, san=san, B_san=B_san):
                                    if firstt:
                                        P.v("pool", "memset", sa[:], 0.0, w=[B_sa])
                                        P.v("pool", "memset", san[:], 0.0, w=[B_san])
                                    P.mm(d_ps[:, 0:Wp], cbs("ntri_ge"), spb[si][:, 0:Wp], start=True, stop=False,
                                         r=[B_c, B_spb[si]], w=[B_d], signal=False)
                                    if not firstt:
                                        P.mm(d_ps[:, 0:Wp], cbs("nones"), sa[:, c0:W], start=False, stop=False,
                                             r=[B_c, B_sa], w=[B_d], signal=False)
                                    if diag:
                                        P.mm(d_ps[:, 0:128], cbs("ident"), cbs("nms"), start=False, stop=False,
                                             r=[B_c], w=[B_d], signal=False)
                                    P.mm(d_ps[:, 0:Wp], K_[0:128, kb * 128:(kb + 1) * 128],
                                         Q_[0:128, q0 + c0:q0 + W],
                                         start=False, stop=True, r=[BK, BQ], w=[B_d])
                                    if not lastt:
                                        P.v("dve", "tensor_tensor", san[:, c0:W], sa[:, c0:W], spb[si][:, 0:Wp],
                                            ALU.add, r=[B_spb[si], B_sa], w=[B_san])

                                def sb_x(Wp=Wp, pi=pi, d_ps=d_ps, B_d=B_d):
                                    P.act(pT[pi][:, 0:Wp], d_ps[:, 0:Wp], AF.Exp, r=[B_d], w=[B_pT[pi]])

                                def sb_C(kb=kb, c0=c0, W=W, Wp=Wp, pi=pi, h=h, o_ps=o_ps, B_o=B_o,
                                         firstt=firstt, lastt=lastt, q0=q0, nc_here=nc_here):
                                    if firstt:
                                        P.mm(o_ps[0:128, 0:W], zero_bf[:, 0:128], zero_bf[:, 0:W], start=True,
                                             stop=False, r=[B_misc], w=[B_o], signal=False, skip=True)
                                    w0_ = min(h * 64, 256)
                                    P.mm(o_ps[0:128, c0:W], vs_all[:, kb, w0_:w0_ + 128], pT[pi][:, 0:Wp],
                                         start=False, stop=lastt, r=[B_vs[kb], B_pT[pi]], w=[B_o],
                                         signal=lastt, skip=True)
                                    if lastt:
                                        head_norm_pre(o_ps, B_o, W, False, ro=h * 64 - min(h * 64, 256))

                                stg = [t_A, fox_exp, fox_C] if fox else [t_A, sb_e, sb_sp, sb_B, sb_x, sb_C]
                                if lastt:
                                    def hn_post(W=W, h=h, q0=q0, fox=fox):
                                        if fox:
                                            head_norm_post(W, fg, h, PW + h * 64, q0, True)
                                        else:
                                            head_norm_post(W, sg, h, PW + NH * 64 + h * 64, q0, False,
                                                           ro=h * 64 - min(h * 64, 256))
                                    stg = stg + [(lambda: None)] * (HN_SKEW - 1) + [hn_post]
                                tiles.append(stg)
                    inflight = []
                    ti = 0
                    while ti < len(tiles) or inflight:
                        if ti < len(tiles):
                            inflight.append([tiles[ti], 0])
                            ti += 1
                            emit_list = list(reversed(inflight))
                        else:
                            emit_list = list(reversed(inflight))
                        for ent in emit_list:
                            ent[0][ent[1]]()
                            ent[1] += 1
                        inflight = [e_ for e_ in inflight if e_[1] < len(e_[0])]
                P.barrier()

            fstack = ExitStack()
            wg = sb(fstack, "wg", [128, 8, DFF], BF16)
            wu = sb(fstack, "wu", [128, 8, DFF], BF16)
            B_wg, B_wu = Buf("wg"), Buf("wu")

            def _ffn_w_gen(layer=layer, wg=wg, wu=wu, B_wg=B_wg, B_wu=B_wu):
                yield from load_weight_gen(wg, B_wg, w_gate[layer], 8, DFF, pool_only=True)
                yield from load_weight_gen(wu, B_wu, w_up[layer], 8, DFF, pool_only=True)
            ffn_gen = _ffn_w_gen()
            with ExitStack() as st:
                wo = sb(st, "wo", [128, 8, D], BF16)
                B_wo = Buf("wo")
                load_weight(wo, B_wo, w_out[layer], 8, D)
                g2 = sb(st, "g2", [128, D], F32)
                B_g2 = Buf("g2")
                P.dma(g2[:], norm2_b[layer], "c1", w=[B_g2])
                yT = [sb(st, f"yT{i}", [128, 8, 512], BF16) for i in range(2)]
                B_yT = [Buf(f"yT{i}") for i in range(2)]
                xt = [sb(st, f"xt{i}", [128, D], F32) for i in range(2)]
                B_xt = [Buf(f"xt{i}") for i in range(2)]
                h1 = [sb(st, f"h1{i}", [128, D], F32) for i in range(2)]
                B_h1 = [Buf(f"h1{i}") for i in range(2)]
                junk = sb(st, "junk", [128, D], BF16)
                B_junk = Buf("junk")
                ss = [sb(st, f"ss{i}", [128, 2], F32) for i in range(2)]
                B_ss = [Buf(f"ss{i}") for i in range(2)]
                abf = [sb(st, f"abf{i}", [128, D], BF16) for i in range(2)]
                B_abf = [Buf(f"abf{i}") for i in range(2)]
                aTg = [sb(st, f"aTg{i}", [128, 8, 512], BF16) for i in range(2)]
                B_aTg = [Buf(f"aTg{i}") for i in range(2)]
                ps_o1 = [ps(st, f"ps_o1{i}", [128, 2, 512]) for i in range(2)]
                B_po1 = [Buf(f"ps_o1{i}") for i in range(2)]
                ps_t = [ps(st, f"ps_t{i}", [128, 8, 128], BF16) for i in range(2)]
                B_pst = [Buf(f"ps_t{i}") for i in range(2)]

                def p3a_loadg(g):
                    i = g % 2
                    g0 = g * 512
                    gw = min(512, L - g0)
                    P.dma(yT[i][:, :, 0:gw], YT[:, g0:g0 + gw].rearrange("(k p) t -> p k t", p=128),
                          f"yT{i}", w=[B_yT[i]])

                def p3a_loadx(t):
                    i = t % 2
                    P.dma(xt[i][:], hsrc[t * 128:(t + 1) * 128, :], f"xt{i}", w=[B_xt[i]])

                p3a_loadg(0)
                p3a_loadx(0)

                def p3a_front(t):
                    i = t % 2
                    g = t // 4
                    gi = g % 2
                    j = t % 4
                    if j == 0 and g + 1 < NG:
                        p3a_loadg(g + 1)
                    if t + 1 < NT:
                        p3a_loadx(t + 1)
                    o1, B_o1 = ps_o1[i], B_po1[i]
                    for half in range(2):
                        for k in range(8):
                            P.mm(o1[:, half, :], yT[gi][:, k, j * 128:(j + 1) * 128],
                                 wo[:, k, half * 512:(half + 1) * 512], start=(k == 0), stop=(k == 7),
                                 r=[B_yT[gi], B_wo], w=[B_o1], signal=(k == 7 and half == 1))
                    P.v("dve", "tensor_tensor", h1[i][:], o1[:].rearrange("p a b -> p (a b)"), xt[i][:], ALU.add,
                        r=[B_o1, B_xt[i]], w=[B_h1[i]])
                    P.dma(Hd[t * 128:(t + 1) * 128, :], h1[i][:], f"h1{i}", r=[B_h1[i]], q="pool")
                    P.act(junk[:], h1[i][:], AF.Square, r=[B_h1[i]], w=[B_junk, B_ss[i]], accum=ss[i][:, 0:1])
                    P.act(ss[i][:, 1:2], ss[i][:, 0:1], AF.Ln, r=[B_ss[i]], w=[B_ss[i]], scale=1.0 / D, bias=epsc[:, 0:1])
                    P.act(ss[i][:, 1:2], ss[i][:, 1:2], AF.Exp, r=[B_ss[i]], w=[B_ss[i]], scale=-0.5)
                    P.v("dve", "scalar_tensor_tensor", abf[i][:], h1[i][:], ss[i][:, 1:2], g2[:],
                        ALU.mult, ALU.mult, r=[B_h1[i], B_ss[i], B_g2], w=[B_abf[i]])

                def p3a_back(t):
                    i = t % 2
                    g = t // 4
                    gi = g % 2
                    j = t % 4
                    for k in range(8):
                        P.tr(ps_t[i][:, k, :], abf[i][:, k * 128:(k + 1) * 128], cbs("ident"),
                             r=[B_abf[i], B_c], w=[B_pst[i]], signal=(k == 7))
                    P.act(aTg[gi][:, :, j * 128:(j + 1) * 128], ps_t[i][:], AF.Copy,
                          r=[B_pst[i]], w=[B_aTg[gi]])
                    if j == 3 or t == NT - 1:
                        gw = (j + 1) * 128
                        g0 = g * 512
                        P.dma(A2T[:, g0:g0 + gw].rearrange("(k p) t -> p k t", p=128), aTg[gi][:, :, 0:gw],
                              f"aTg{gi}", r=[B_aTg[gi]], q="pool")

                p3a_front(0)
                for t in range(NT):
                    if t + 1 < NT:
                        p3a_front(t + 1)
                    p3a_back(t)
                    next(ffn_gen, None)
                    next(ffn_gen, None)
                for _ in ffn_gen:
                    pass
            P.barrier()

            with ExitStack() as st:
                wd = sb(st, "wd", [128, 22, D], BF16)
                B_wd = Buf("wd")
                wd_gen = load_weight_gen(wd, B_wd, w_down[layer], 22, D, pool_only=True)
                a2 = [sb(st, f"a2{i}", [128, 8, 512], BF16) for i in range(2)]
                B_a2 = [Buf(f"a2{i}") for i in range(2)]
                gT = sb(st, "gT", [128, 22, 512], BF16)
                B_gT = [Buf(f"gT{f}") for f in range(22)]
                sgl = [sb(st, f"sgl{i}", [128, 512], BF16) for i in range(2)]
                B_sgl = [Buf(f"sgl{i}") for i in range(2)]
                xt = [sb(st, f"xt{i}", [128, D], F32) for i in range(2)]
                B_xt = [Buf(f"xt{i}") for i in range(2)]
                h2 = xt
                B_h2 = B_xt
                ps_g = [ps(st, f"ps_g{i}", [128, 512]) for i in range(2)]
                B_pg = [Buf(f"ps_g{i}") for i in range(2)]
                ps_u = [ps(st, f"ps_u{i}", [128, 512]) for i in range(2)]
                B_pu = [Buf(f"ps_u{i}") for i in range(2)]
                ps_o2 = [ps(st, f"ps_o2{i}", [128, 2, 512]) for i in range(2)]
                B_po2 = [Buf(f"ps_o2{i}") for i in range(2)]

                def p3b_loadg(g):
                    i = g % 2
                    g0 = g * 512
                    gw = min(512, L - g0)
                    P.dma(a2[i][:, :, 0:gw], A2T[:, g0:g0 + gw].rearrange("(k p) t -> p k t", p=128),
                          f"a2{i}", w=[B_a2[i]])

                def p3b_loadx(t):
                    i = t % 2
                    P.dma(xt[i][:], Hd[t * 128:(t + 1) * 128, :], f"xt{i}", w=[B_xt[i]])

                p3b_loadg(0)
                p3b_loadx(0)
                fc = 0
                for g in range(NG):
                    gi = g % 2
                    g0 = g * 512
                    gw = min(512, L - g0)
                    if g + 1 < NG:
                        p3b_loadg(g + 1)
                    for f in range(22):
                        fi = fc % 2
                        fc += 1
                        for k in range(8):
                            P.mm(ps_g[fi][:, 0:gw], wg[:, k, f * 128:(f + 1) * 128], a2[gi][:, k, 0:gw],
                                 start=(k == 0), stop=(k == 7), r=[B_wg, B_a2[gi]], w=[B_pg[fi]], signal=(k == 7))
                        for k in range(8):
                            P.mm(ps_u[fi][:, 0:gw], wu[:, k, f * 128:(f + 1) * 128], a2[gi][:, k, 0:gw],
                                 start=(k == 0), stop=(k == 7), r=[B_wu, B_a2[gi]], w=[B_pu[fi]], signal=(k == 7))
                        P.act(sgl[fi][:, 0:gw], ps_g[fi][:, 0:gw], AF.Silu, r=[B_pg[fi]], w=[B_sgl[fi]])
                        P.v("dve", "tensor_tensor", gT[:, f, 0:gw], sgl[fi][:, 0:gw], ps_u[fi][:, 0:gw], ALU.mult,
                            r=[B_sgl[fi], B_pu[fi]], w=[B_gT[f]])
                        if g == 0:
                            next(wd_gen, None)
                            next(wd_gen, None)
                    if g == 0:
                        for _ in wd_gen:
                            pass
                    for j in range(gw // 128):
                        t = g * 4 + j
                        i = t % 2
                        if t + 1 < NT:
                            p3b_loadx(t + 1)
                        o2, B_o2 = ps_o2[i], B_po2[i]
                        for half in range(2):
                            for f in range(22):
                                P.mm(o2[:, half, :], gT[:, f, j * 128:(j + 1) * 128],
                                     wd[:, f, half * 512:(half + 1) * 512], start=(f == 0), stop=(f == 21),
                                     r=[B_gT[f], B_wd], w=[B_o2], signal=(f == 21 and half == 1))
                        P.v("dve", "tensor_tensor", h2[i][:], o2[:].rearrange("p a b -> p (a b)"), xt[i][:],
                            ALU.add, r=[B_o2], w=[B_h2[i]])
                        P.dma(Hd[t * 128:(t + 1) * 128, :], h2[i][:], f"h2{i}", r=[B_h2[i]], q="pool")
            P.barrier()
            fstack.close()

        with ExitStack() as st:
            gfin = sb(st, "gfin", [128, D], F32)
            B_gf = Buf("gfin")
            P.dma(gfin[:], final_b[:, :], "c1", w=[B_gf])
            junk = sb(st, "junk", [128, D], BF16)
            B_junk = Buf("junk")
            NB4 = 4
            xt = [sb(st, f"xt{i}", [128, D], F32) for i in range(NB4)]
            B_xt = [Buf(f"xt{i}") for i in range(NB4)]
            ss = [sb(st, f"ss{i}", [128, 2], F32) for i in range(NB4)]
            B_ss = [Buf(f"ss{i}") for i in range(NB4)]

            def p4_load(t):
                i = t % NB4
                P.dma(xt[i][:], Hd[t * 128:(t + 1) * 128, :], f"xt{i}", w=[B_xt[i]])

            for t in range(min(2, NT)):
                p4_load(t)
            for t in range(NT):
                i = t % NB4
                if t + 2 < NT:
                    p4_load(t + 2)
                P.act(junk[:], xt[i][:], AF.Square, r=[B_xt[i]], w=[B_junk, B_ss[i]], accum=ss[i][:, 0:1])
                P.act(ss[i][:, 1:2], ss[i][:, 0:1], AF.Ln, r=[B_ss[i]], w=[B_ss[i]], scale=1.0 / D, bias=epsc[:, 0:1])
                P.act(ss[i][:, 1:2], ss[i][:, 1:2], AF.Exp, r=[B_ss[i]], w=[B_ss[i]], scale=-0.5)
                P.v("dve", "scalar_tensor_tensor", xt[i][:], xt[i][:], ss[i][:, 1:2], gfin[:],
                    ALU.mult, ALU.mult, r=[B_ss[i], B_gf], w=[B_xt[i]])
                if L == LFULL:
                    r0 = t * 128 - NMETA
                    lo = max(r0, 0)
                    hi_ = min(r0 + 128, NOUT)
                    if hi_ > lo:
                        P.dma(out[lo:hi_, :], xt[i][lo - r0:hi_ - r0, :], f"ot{i}", r=[B_xt[i]], q="pool")
                else:
                    P.dma(out[t * 128:(t + 1) * 128, :], xt[i][:], f"ot{i}", r=[B_xt[i]], q="pool")
        P.barrier()

        P.emit()
    return nc


def host_inputs(xb, meta_tokens, norm1, w_in, forget_bias, pool_w, pool_scale, fox_out_gain,
                sb_out_gain, w_out, norm2, w_gate, w_up, w_down, final_norm, L=LFULL):
    f = np.float32
    depth = w_in.shape[0]
    xpad = np.zeros((L, D), f)
    xpad[:NMETA] = meta_tokens
    n = min(L - NMETA, xb.shape[0])
    xpad[NMETA:NMETA + n] = xb[:n]
    m = {
        "xpad": xpad,
        "w_in": np.ascontiguousarray(w_in, f),
        "w_out": np.ascontiguousarray(w_out, f),
        "w_gate": np.ascontiguousarray(w_gate, f),
        "w_up": np.ascontiguousarray(w_up, f),
        "w_down": np.ascontiguousarray(w_down, f),
        "pool_w": np.ascontiguousarray(np.transpose(pool_w, (0, 2, 1, 3)), f),
        "norm1_b": np.ascontiguousarray(np.broadcast_to(norm1[:, None, :], (depth, 128, D)), f),
        "norm2_b": np.ascontiguousarray(np.broadcast_to(norm2[:, None, :], (depth, 128, D)), f),
        "final_b": np.ascontiguousarray(np.broadcast_to(final_norm[None, :], (128, D)), f),
        "fb_b": np.ascontiguousarray(np.broadcast_to(forget_bias[:, None, :], (depth, 128, NH)), f),
        "pscale_c": np.ascontiguousarray(np.transpose(pool_scale.reshape(depth, 4, 64), (0, 2, 1)), f),
        "fgain_c": np.ascontiguousarray(np.transpose(fox_out_gain.reshape(depth, NH, 64), (0, 2, 1)), f),
        "sgain_c": np.ascontiguousarray(np.transpose(sb_out_gain.reshape(depth, NH, 64), (0, 2, 1)), f),
        "consts": make_consts(),
    }
    return m


_NC_CACHE = {}


def kernel(x, meta_tokens, norm1, w_in, forget_bias, pool_w, pool_scale, fox_out_gain,
           sb_out_gain, w_out, norm2, w_gate, w_up, w_down, final_norm):
    args = [np.asarray(a) for a in (meta_tokens, norm1, w_in, forget_bias, pool_w, pool_scale,
                                    fox_out_gain, sb_out_gain, w_out, norm2, w_gate, w_up, w_down,
                                    final_norm)]
    x = np.asarray(x)
    if "nc" not in _NC_CACHE:
        _NC_CACHE["nc"] = build_program()
    nc = _NC_CACHE["nc"]
    in_maps = []
    for c in range(8):
        b = c % BATCH
        in_maps.append(host_inputs(x[b], *args))
    res = run_bass_kernel_spmd(nc, in_maps, core_ids=list(range(8)))
    outs = [res.results[b]["out"] for b in range(BATCH)]
    return np.stack(outs, axis=0).astype(np.float32)
```
